# Optimizing a Trainium2 kernel written in Bass

```python
import jax, jax.numpy as jnp
from jax import lax
import numpy as np

D_MODEL = 4096
BATCH = 2
SEQ = 8192
DEPTH = 2

MIX_WIDTH = D_MODEL // 4
N_BRANCH = 3
HEAD_DIM = 128
CHUNK = 128
A_GROUP_DIM = 128
A_GROUPS = MIX_WIDTH // A_GROUP_DIM
B_PATTERNS = ((128, 1), (512, 4), (2048, 16))
B_HEADS_PER_GROUP = MIX_WIDTH // HEAD_DIM
B_HEADS = B_HEADS_PER_GROUP * len(B_PATTERNS)
ATTN_BLOCK = 128
ROT_DIM = HEAD_DIM // 4
ROPE_THETA = 500000.0
CONV_WIDTH = 3
A_COLS = 2 * MIX_WIDTH
B_COLS = 3 * B_HEADS * HEAD_DIM
C_COLS = 3 * MIX_WIDTH
IN_COLS = A_COLS + B_COLS + C_COLS
PEER_HEADS = 8
N_KEYS = 128
N_EXPERTS = N_KEYS * N_KEYS
PEER_TOPK = 16
D_KEY = 128
PEER_TOK_BLOCK = 32
ALPHA = (2 * DEPTH) ** 0.25
BETA = (8 * DEPTH) ** -0.25
LN_EPS = 1e-5
ADA_INIT = 0.25
POS_OFFSET_MAX = 1024

kernel_name = "hybrid_gmlp_dilattn_shortconv_peer_deepnorm"


def layer_norm(x, g, b):
    xf = x.astype(jnp.float32)
    mu = jnp.mean(xf, axis=-1, keepdims=True)
    var = jnp.mean(jnp.square(xf - mu), axis=-1, keepdims=True)
    return ((xf - mu) * lax.rsqrt(var + LN_EPS)).astype(x.dtype) * g + b


def partial_rope(t, positions):
    half = ROT_DIM // 2
    inv_freq = ROPE_THETA ** (-jnp.arange(half, dtype=jnp.float32) / half)
    ang = positions.astype(jnp.float32)[..., None] * inv_freq
    cos = jnp.cos(ang)[:, :, None, :].astype(t.dtype)
    sin = jnp.sin(ang)[:, :, None, :].astype(t.dtype)
    x1 = t[..., :half]
    x2 = t[..., half:ROT_DIM]
    return jnp.concatenate([x1 * cos - x2 * sin, x2 * cos + x1 * sin, t[..., ROT_DIM:]], axis=-1)


def chunked_gmlp(uv, ln_g, ln_b, w_sp, b_sp):
    Bsz, S, _ = uv.shape
    u, v = jnp.split(uv, 2, axis=-1)
    v = layer_norm(v, ln_g, ln_b)
    v = v.reshape(Bsz, S // CHUNK, CHUNK, A_GROUPS, A_GROUP_DIM)
    causal = jnp.tril(jnp.ones((CHUNK, CHUNK), dtype=bool))
    w = jnp.where(causal, w_sp, 0)
    mixed = jnp.einsum("gts,bnsgc->bntgc", w, v) + b_sp.T[:, :, None]
    return u * mixed.reshape(Bsz, S, MIX_WIDTH)


def dilated_window_attention(q, k, v, window, dilation):
    Bsz, S, H, Dh = q.shape
    span = window // dilation
    period = ATTN_BLOCK * dilation
    Sp = -(-S // period) * period
    nb = Sp // period

    def blocks(t):
        t = jnp.pad(t, ((0, 0), (0, Sp - S), (0, 0), (0, 0)))
        return t.reshape(Bsz, nb, ATTN_BLOCK, dilation, H, Dh)

    def with_prev(t):
        prev = jnp.pad(t[:, :-1], ((0, 0), (1, 0), (0, 0), (0, 0), (0, 0), (0, 0)))
        return jnp.concatenate([prev, t], axis=2)

    qb = blocks(q)
    kk = with_prev(blocks(k))
    vv = with_prev(blocks(v))
    s = jnp.einsum("bnqrhd,bnkrhd->bnrhqk", qb, kk).astype(jnp.float32) * (Dh ** -0.5)
    qi = jnp.arange(ATTN_BLOCK)[:, None]
    kj = jnp.arange(2 * ATTN_BLOCK)[None, :]
    dist = qi + ATTN_BLOCK - kj
    band = (dist >= 0) & (dist <= span)
    exists = (jnp.arange(nb)[:, None, None] > 0) | (kj >= ATTN_BLOCK)[None]
    mask = (band[None] & exists)[None, :, None, None]
    s = jnp.where(mask, s, -jnp.inf)
    lse = jax.nn.logsumexp(s, axis=-1)
    p = jnp.exp(s - lse[..., None]).astype(v.dtype)
    o = jnp.einsum("bnrhqk,bnkrhd->bnqrhd", p, vv).reshape(Bsz, Sp, H, Dh)[:, :S]
    lse = jnp.transpose(lse, (0, 1, 4, 2, 3)).reshape(Bsz, Sp, H)[:, :S]
    return o, lse


def dilated_mixture_attention(qkv, positions):
    Bsz, S, _ = qkv.shape
    qkv = qkv.reshape(Bsz, S, 3, B_HEADS, HEAD_DIM)
    q = partial_rope(qkv[:, :, 0], positions)
    k = partial_rope(qkv[:, :, 1], positions)
    v = qkv[:, :, 2]
    outs, lses = [], []
    for g, (window, dilation) in enumerate(B_PATTERNS):
        hs = slice(g * B_HEADS_PER_GROUP, (g + 1) * B_HEADS_PER_GROUP)
        o, l = dilated_window_attention(q[:, :, hs], k[:, :, hs], v[:, :, hs], window, dilation)
        outs.append(o)
        lses.append(l)
    o = jnp.stack(outs, axis=2)
    w = jax.nn.softmax(jnp.stack(lses, axis=2), axis=2)
    out = jnp.einsum("bsgh,bsghd->bshd", w.astype(o.dtype), o)
    return out.reshape(Bsz, S, MIX_WIDTH)


def short_conv_mixer(bcx, conv_w):
    gate_b, gate_c, xin = jnp.split(bcx, 3, axis=-1)
    z = gate_c * xin
    y = lax.conv_general_dilated(
        z, conv_w[:, None, :], window_strides=(1,), padding=[(CONV_WIDTH - 1, 0)],
        dimension_numbers=("NWC", "WIO", "NWC"), feature_group_count=MIX_WIDTH)
    return gate_b * y


def mixer_sublayer(h, positions, w_in, w_gate, b_gate, ln_v_g, ln_v_b, w_sp, b_sp,
                   conv_w, w_branch, w_o):
    proj = h @ w_in
    a_uv, b_qkv, c_bcx = jnp.split(proj, [A_COLS, A_COLS + B_COLS], axis=-1)
    branches = (
        chunked_gmlp(jax.nn.gelu(a_uv), ln_v_g, ln_v_b, w_sp, b_sp),
        dilated_mixture_attention(b_qkv, positions),
        short_conv_mixer(c_bcx, conv_w),
    )
    merged = jnp.zeros_like(h)
    for g in range(N_BRANCH):
        gate = jax.nn.sigmoid(h @ w_gate[g] + b_gate[g])
        merged = merged + gate * (branches[g] @ w_branch[g])
    return merged @ w_o


def peer(h, w_pq, sub_keys, w_u, w_v):
    Bsz, S, D = h.shape
    q = (h @ w_pq).reshape(Bsz, S, PEER_HEADS, 2, D_KEY // 2)
    s = jnp.einsum("bshpd,hpnd->bshpn", q, sub_keys).astype(jnp.float32)
    s_top, i_top = lax.top_k(s, PEER_TOPK)
    cand_s = (s_top[..., 0, :, None] + s_top[..., 1, None, :]).reshape(Bsz, S, PEER_HEADS, PEER_TOPK * PEER_TOPK)
    cand_i = (i_top[..., 0, :, None] * N_KEYS + i_top[..., 1, None, :]).reshape(Bsz, S, PEER_HEADS, PEER_TOPK * PEER_TOPK)
    best_s, best_pos = lax.top_k(cand_s, PEER_TOPK)
    idx = jnp.take_along_axis(cand_i, best_pos, axis=-1)
    gate = jax.nn.softmax(best_s, axis=-1).astype(h.dtype)
    nblk = (Bsz * S) // PEER_TOK_BLOCK
    hb = h.reshape(nblk, PEER_TOK_BLOCK, D)
    ib = idx.reshape(nblk, PEER_TOK_BLOCK, PEER_HEADS * PEER_TOPK)
    gb = gate.reshape(nblk, PEER_TOK_BLOCK, PEER_HEADS * PEER_TOPK)

    def expert_block(args):
        xt, it, gt = args
        act = jax.nn.gelu(jnp.einsum("td,tkd->tk", xt, w_u[it]))
        return jnp.einsum("tk,tkd->td", gt * act, w_v[it])

    return lax.map(expert_block, (hb, ib, gb)).reshape(Bsz, S, D)


def setup_inputs(seed: int = 0) -> dict:
    key = jax.random.key(seed)
    ks = iter(jax.random.split(key, 32))
    L, D = DEPTH, D_MODEL

    def nrm(shape, scale):
        return jax.random.normal(next(ks), shape, jnp.float32) * scale

    x = nrm((BATCH, SEQ, D), 1.0)
    c = nrm((BATCH, D), 1.0)
    start = jax.random.randint(next(ks), (BATCH, 1), 0, POS_OFFSET_MAX, dtype=jnp.int32)
    positions = start + jnp.arange(SEQ, dtype=jnp.int32)[None, :]
    return {
        "x": x,
        "c": c,
        "positions": positions,
        "w_ada": nrm((L, D, 6 * D), ADA_INIT * D ** -0.5),
        "b_ada": nrm((L, 6 * D), 0.01),
        "w_in": nrm((L, D, IN_COLS), D ** -0.5),
        "w_gate": nrm((L, N_BRANCH, D, D), D ** -0.5),
        "b_gate": nrm((L, N_BRANCH, D), 0.01),
        "ln_v_g": 1.0 + nrm((L, MIX_WIDTH), 0.02),
        "ln_v_b": nrm((L, MIX_WIDTH), 0.02),
        "w_sp": nrm((L, A_GROUPS, CHUNK, CHUNK), CHUNK ** -0.5),
        "b_sp": 1.0 + nrm((L, A_GROUPS, CHUNK), 0.1),
        "conv_w": nrm((L, CONV_WIDTH, MIX_WIDTH), CONV_WIDTH ** -0.5),
        "w_branch": nrm((L, N_BRANCH, MIX_WIDTH, D), MIX_WIDTH ** -0.5),
        "w_o": nrm((L, D, D), BETA * D ** -0.5),
        "ln1_g": 1.0 + nrm((L, D), 0.02),
        "ln1_b": nrm((L, D), 0.02),
        "w_pq": nrm((L, D, PEER_HEADS * D_KEY), D ** -0.5),
        "sub_keys": nrm((L, PEER_HEADS, 2, N_KEYS, D_KEY // 2), (D_KEY // 2) ** -0.5),
        "w_u": nrm((L, N_EXPERTS, D), D ** -0.5),
        "w_v": nrm((L, N_EXPERTS, D), BETA),
        "ln2_g": 1.0 + nrm((L, D), 0.02),
        "ln2_b": nrm((L, D), 0.02),
    }


def reference(x, c, positions, w_ada, b_ada, w_in, w_gate, b_gate, ln_v_g, ln_v_b, w_sp, b_sp,
              conv_w, w_branch, w_o, ln1_g, ln1_b, w_pq, sub_keys, w_u, w_v, ln2_g, ln2_b):
    c_act = jax.nn.silu(c)
    for l in range(DEPTH):
        mod = c_act @ w_ada[l] + b_ada[l]
        sh1, sc1, g1, sh2, sc2, g2 = [m[:, None, :] for m in jnp.split(mod, 6, axis=-1)]
        h = x * (1 + sc1) + sh1
        y = mixer_sublayer(h, positions, w_in[l], w_gate[l], b_gate[l], ln_v_g[l], ln_v_b[l],
                           w_sp[l], b_sp[l], conv_w[l], w_branch[l], w_o[l])
        x = layer_norm(ALPHA * x + g1 * y, ln1_g[l], ln1_b[l])
        h = x * (1 + sc2) + sh2
        y = peer(h, w_pq[l], sub_keys[l], w_u[l], w_v[l])
        x = layer_norm(ALPHA * x + g2 * y, ln2_g[l], ln2_b[l])
    return x
```

```python
import numpy as np
import concourse.bass as bass
import concourse.mybir as mybir
from concourse.bass_utils import run_bass_kernel_spmd

F32 = mybir.dt.float32
BF16 = mybir.dt.bfloat16
I32 = mybir.dt.int32
AF = mybir.ActivationFunctionType
ALU = mybir.AluOpType

D = 4096
KC = 32
SEQ = 8192
DEPTH = 2
MIXW = 1024
NCORES = 8
CH = 2048
ALPHA = (2 * DEPTH) ** 0.25
LN_EPS = 1e-5
NEG = -1.0e30
TWO_PI = 6.283185307179586


class Res:
    def __init__(self, name, t=None):
        self.name = name
        self.t = t
        self.w = {}
        self.r = {}
        self.lsem = None
        self.ssem = None


class Sched:
    def __init__(self, nc):
        self.nc = nc
        self.eng = {'pe': nc.tensor, 'act': nc.scalar, 'dve': nc.vector, 'pool': nc.gpsimd, 'sp': nc.sync}
        self.sems = {}
        self.cnt = {}
        self.seen = {e: {} for e in self.eng}
        self.ctxs = []
        self.marks = []
        self.free_dsem = []
        self.ndsem = 0
        self.n_inst = 0
        for e in self.eng:
            self._mksem(e)

    def _mksem(self, key):
        cm = self.nc.semaphore("s_" + key)
        h = cm.__enter__()
        self.sems[key] = h
        self.cnt[key] = 0
        return h

    def _dsem(self):
        if self.free_dsem:
            return self.free_dsem.pop()
        key = "d%d" % self.ndsem
        self.ndsem += 1
        self._mksem(key)
        return key

    def push(self):
        self.marks.append((len(self.ctxs), []))

    def pop(self):
        self.barrier()
        n, sems = self.marks.pop()
        while len(self.ctxs) > n:
            cm = self.ctxs.pop()
            cm.__exit__(None, None, None)
        self.free_dsem.extend(sems)

    def _alloc(self, cm, name):
        t = cm.__enter__()
        self.ctxs.append(cm)
        r = Res(name, t)
        r.sched = self
        return r

    def sb(self, name, shape, dt):
        self.uid = getattr(self, "uid", 0) + 1
        return self._alloc(self.nc.sbuf_tensor(name + "_u%d" % self.uid, shape, dt), name)

    def ps(self, name, shape, dt):
        return self._alloc(self.nc.psum_tensor(name, shape, dt), name)

    def _wait(self, e, deps, raw=None):
        eng = self.eng[e]
        seen = self.seen[e]
        for k, v in deps.items():
            if k == e:
                if e == 'pe' or raw is None or k not in raw:
                    continue
                v = raw[k]
            if seen.get(k, 0) >= v:
                continue
            eng.wait_ge(self.sems[k], v)
            self.n_inst += 1
            seen[k] = v

    @staticmethod
    def _add(deps, k, v):
        if deps.get(k, 0) < v:
            deps[k] = v

    def _raw(self, reads):
        raw = {}
        for r in reads:
            for k, v in r.w.items():
                self._add(raw, k, v)
        return raw

    def _deps(self, reads, writes):
        deps = {}
        for r in reads:
            for k, v in r.w.items():
                self._add(deps, k, v)
        for w in writes:
            for k, v in w.w.items():
                self._add(deps, k, v)
            for k, v in w.r.items():
                self._add(deps, k, v)
        return deps

    def _mark(self, k, v, reads, writes, acc=False):
        for r in reads:
            if r.r.get(k, 0) < v:
                r.r[k] = v
        for w in writes:
            if acc:
                if w.w.get(k, 0) < v:
                    w.w[k] = v
            else:
                w.w = {k: v}
                w.r = {}

    def op(self, e, fn, reads=(), writes=()):
        self._wait(e, self._deps(reads, writes), self._raw(reads))
        inst = fn()
        self.cnt[e] += 1
        inst.then_inc(self.sems[e], 1)
        self._mark(e, self.cnt[e], reads, writes)
        self.n_inst += 1
        return inst

    def mm_group(self, fns, reads, writes):
        self._wait('pe', self._deps(reads, writes))
        inst = None
        for fn in fns:
            inst = fn()
        self.cnt['pe'] += 1
        inst.then_inc(self.sems['pe'], 1)
        self._mark('pe', self.cnt['pe'], reads, writes)
        self.n_inst += len(fns)

    def load(self, q, dst, out_ap, in_ap, reads=(), **kw):
        self._wait(q, self._deps(reads, [dst]))
        if dst.lsem is None:
            dst.lsem = self._dsem()
            if self.marks:
                self.marks[-1][1].append(dst.lsem)
        k = dst.lsem
        inst = self.eng[q].dma_start(out=out_ap, in_=in_ap, **kw)
        self.cnt[k] += 16
        inst.then_inc(self.sems[k], 16)
        self._mark(k, self.cnt[k], reads, [dst])
        self.n_inst += 1

    def store(self, q, dst, out_ap, src, in_ap, extra=(), **kw):
        self._wait(q, self._deps([src] + list(extra), [dst]))
        if src.ssem is None:
            src.ssem = self._dsem()
            if self.marks:
                self.marks[-1][1].append(src.ssem)
        k = src.ssem
        inst = self.eng[q].dma_start(out=out_ap, in_=in_ap, **kw)
        self.cnt[k] += 16
        inst.then_inc(self.sems[k], 16)
        self._mark(k, self.cnt[k], [src] + list(extra), [dst], acc=True)
        self.n_inst += 1

    def barrier(self):
        deps = {k: v for k, v in self.cnt.items() if v > 0}
        for e in self.eng:
            self._wait(e, deps)

    def finish(self):
        self.barrier()
        while self.ctxs:
            self.ctxs.pop().__exit__(None, None, None)


class Builder:
    def __init__(self, N, H, nlayers, debug=()):
        self.N, self.H, self.T = N, H, N + H
        self.debug = set(debug)
        self.nc = bass.Bass("TRN2", target_bir_lowering=False)
        self.S = Sched(self.nc)
        self.dram_res = {}
        self.ndram = 0

    def din(self, name, shape, dt=F32):
        return self.nc.dram_tensor(name, list(shape), dt, kind="ExternalInput").ap()

    def dout(self, name, shape, dt=F32):
        ap = self.nc.dram_tensor(name, list(shape), dt, kind="ExternalOutput").ap()
        self.dram_res[name] = Res(name)
        return ap

    def scratch(self, name, shape, dt):
        kind = "ExternalOutput" if name in self.debug else "Internal"
        ap = self.nc.dram_tensor(name, list(shape), dt, kind=kind).ap()
        self.dram_res[name] = Res(name)
        return ap

    def R(self, name):
        return self.dram_res[name]

    def setup_consts(self, consts, flags):
        S, nc = self.S, self.nc
        self.ident_f = S.sb("ident_f", [128, 128], F32)
        self.ident_b = S.sb("ident_b", [128, 128], BF16)
        self.tri_ge = S.sb("tri_ge", [128, 128], F32)
        self.tri_le = S.sb("tri_le", [128, 128], F32)
        self.ones_b = S.sb("ones_b", [128, 128], BF16)
        self.ones_f = S.sb("ones_f", [1, 128], F32)
        self.cst = S.sb("cst", [128, 256], F32)
        self.flg = S.sb("flg", [128, 4], F32)
        self.epsc = S.sb("epsc", [128, 1], F32)
        S.load('sp', self.cst, self.cst.t[:], consts[:, :])
        S.load('sp', self.flg, self.flg.t[:], flags[:, :])
        S.op('pool', lambda: nc.gpsimd.memset(self.epsc.t[:], LN_EPS), writes=[self.epsc])
        S.op('pool', lambda: nc.gpsimd.memset(self.ones_b.t[:], 1.0), writes=[self.ones_b])
        S.op('pool', lambda: nc.gpsimd.memset(self.ones_f.t[:], 1.0), writes=[self.ones_f])
        for t, op, dt in ((self.ident_f, ALU.is_equal, None), (self.tri_ge, ALU.is_ge, None), (self.tri_le, ALU.is_ge, -1)):
            S.op('pool', lambda t=t: nc.gpsimd.memset(t.t[:], 1.0), writes=[t])
            if dt is None:
                S.op('pool', lambda t=t, op=op: nc.gpsimd.affine_select(out=t.t[:], in_=t.t[:], pattern=[[-1, 128]], compare_op=op,
                                                                    fill=0.0, base=0, channel_multiplier=1), reads=[t], writes=[t])
            else:
                S.op('pool', lambda t=t, op=op: nc.gpsimd.affine_select(out=t.t[:], in_=t.t[:], pattern=[[1, 128]], compare_op=op,
                                                                    fill=0.0, base=0, channel_multiplier=-1), reads=[t], writes=[t])
        S.op('dve', lambda: nc.vector.tensor_copy(self.ident_b.t[:], self.ident_f.t[:]), reads=[self.ident_f], writes=[self.ident_b])
        self.maskpair = []
        for s in range(4):
            mp = S.sb("maskpair%d" % s, [128, 2, 128], BF16)
            S.op('dve', lambda mp=mp, s=s: nc.vector.tensor_scalar(mp.t[:, 0, :], self.tri_ge.t[:], self.flg.t[:, s:s + 1], None, op0=ALU.mult),
                 reads=[self.tri_ge, self.flg], writes=[mp])
            S.op('dve', lambda mp=mp: nc.vector.tensor_copy(mp.t[:, 1, :], self.tri_le.t[:]), reads=[self.tri_le, mp], writes=[mp])
            self.maskpair.append(mp)
        self.psb = [S.ps("psb%d" % i, [128, 512], F32) for i in range(8)]
        self.psi = 0

    def bank(self, lo=0, hi=8):
        n = hi - lo
        if not hasattr(self, 'psis'):
            self.psis = {}
        i = self.psis.get((lo, hi), 0)
        self.psis[(lo, hi)] = i + 1
        return self.psb[lo + (i % n)]

    def phase_mod(self, L, cvec, w_ada, b_ada):
        S, nc = self.S, self.nc
        modrow = self.scratch("modrow%d" % L, [1, 6 * D], F32)
        mr = self.R("modrow%d" % L)
        S.push()
        ct = S.sb("ct", [128, KC], F32)
        cs = S.sb("cs", [128, KC], F32)
        S.load('sp', ct, ct.t[:], cvec.rearrange("(p c) -> p c", c=KC))
        S.op('act', lambda: nc.scalar.activation(cs.t[:], ct.t[:], AF.Silu), reads=[ct], writes=[cs])
        CW = 256
        wv = w_ada.rearrange("(p c) n -> p c n", c=KC)
        wb = [S.sb("wada%d" % i, [128, KC, CW], F32) for i in range(2)]
        bb = [S.sb("bada%d" % i, [1, CW], F32) for i in range(2)]
        ob = [S.sb("oada%d" % i, [1, CW], F32) for i in range(2)]
        nchunk = 6 * D // CW
        for j in range(nchunk):
            w, b, o = wb[j % 2], bb[j % 2], ob[j % 2]
            S.load('sp' if j % 2 == 0 else 'act', w, w.t[:], wv[:, :, j * CW:(j + 1) * CW])
            S.load('sp', b, b.t[:], b_ada[:, j * CW:(j + 1) * CW])
            pb = self.bank()
            S.mm_group([(lambda c=c, w=w, pb=pb: nc.tensor.matmul(pb.t[0:1, 0:CW], cs.t[:, c:c + 1], w.t[:, c, :], start=(c == 0), stop=(c == KC - 1)))
                        for c in range(KC)], reads=[cs, w], writes=[pb])
            S.op('dve', lambda pb=pb, b=b, o=o: nc.vector.tensor_tensor(o.t[:], pb.t[0:1, 0:CW], b.t[:], op=ALU.add), reads=[pb, b], writes=[o])
            S.store('sp', mr, modrow[:, j * CW:(j + 1) * CW], o, o.t[:])
        S.pop()
        return modrow

    def load_modT(self, L, modrow):
        S, nc = self.S, self.nc
        mt = S.sb("modT%d" % L, [128, 6, KC], F32)
        S.load('sp', mt, mt.t[:], modrow.rearrange("o (s p c) -> p (o s) c", s=6, c=KC), reads=[self.R("modrow%d" % L)])
        for s in (1, 4):
            S.op('dve', lambda s=s: nc.vector.tensor_scalar(mt.t[:, s, :], mt.t[:, s, :], 1.0, None, op0=ALU.add), reads=[mt], writes=[mt])
        return mt

    def phase_hT(self, name, x_ap, x_res, ntok, mt, s_shift, s_scale):
        S, nc = self.S, self.nc
        hT = self.scratch(name, [128, KC, ntok], BF16)
        hr = self.R(name)
        S.push()
        xb = [S.sb("xb%d" % i, [128, D], F32) for i in range(2)]
        hb = [S.sb("hb%d" % i, [128, KC, 512], BF16) for i in range(2)]
        hbA = [Res("hbA%d" % i, hb[i].t) for i in range(2)]
        nt = ntok // 128
        for t in range(nt):
            x = xb[t % 2]
            h = hb[(t // 4) % 2]
            hA = hbA[(t // 4) % 2]
            S.load('sp' if t % 2 == 0 else 'act', x, x.t[:], x_ap[t * 128:(t + 1) * 128, :], reads=[x_res] if x_res is not None else [])
            xv = x.t[:].rearrange("t (p c) -> t c p", c=KC)
            for cg in range(KC // 4):
                pb = self.bank()
                S.mm_group([(lambda j=j, pb=pb: nc.tensor.transpose(pb.t[:, j * 128:(j + 1) * 128], xv[:, cg * 4 + j, :], self.ident_f.t[:]))
                            for j in range(4)], reads=[x, self.ident_f], writes=[pb])
                for j in range(4):
                    c = cg * 4 + j
                    o = h.t[:, c, (t % 4) * 128:(t % 4 + 1) * 128]
                    i_ = pb.t[:, j * 128:(j + 1) * 128]
                    if cg % 2 == 0:
                        S.op('dve', lambda o=o, i_=i_, c=c: nc.vector.tensor_scalar(o, i_, mt.t[:, s_scale, c:c + 1], mt.t[:, s_shift, c:c + 1],
                                                                                     op0=ALU.mult, op1=ALU.add), reads=[pb, mt], writes=[h])
                    else:
                        S.op('act', lambda o=o, i_=i_, c=c: nc.scalar.activation(o, i_, AF.Identity, bias=mt.t[:, s_shift, c:c + 1],
                                                                                scale=mt.t[:, s_scale, c:c + 1]), reads=[pb, mt], writes=[hA])
            if t % 4 == 3:
                t0 = (t // 4) * 512
                S.store('sp', hr, hT[:, :, t0:t0 + 512], h, h.t[:], extra=[hA])
        S.pop()
        return hT

    def gemm(self, mode, inT, in_res, kc, tiles, w3, jobs, wcols=512, wq='pool'):
        S, nc = self.S, self.nc
        S.push()
        inb = [S.sb("inb%d" % i, [128, kc, 512], BF16) for i in range(2)]
        wb = [S.sb("wslab%d" % i, [128, kc, wcols], BF16) for i in range(2)]
        wi = 0
        for st in range(0, len(tiles), 2):
            halves = tiles[st:st + 2]
            for hi_, t0 in enumerate(halves):
                S.load('sp', inb[hi_], inb[hi_].t[:], inT[:, :, t0:t0 + 512], reads=[in_res])
            for job in jobs:
                act = [(hi_, t0) for hi_, t0 in enumerate(halves) if job['want'](t0)]
                if not act:
                    continue
                c0, ncols = job['c0'], job['ncols']
                w = wb[wi % 2]
                wi += 1
                S.load(wq, w, w.t[:, :, 0:ncols], w3[:, :, c0:c0 + ncols])
                if mode == 'B':
                    for m in range(ncols // 128):
                        for hi_, t0 in act:
                            pb = self.bank(0, 4)
                            S.mm_group([(lambda k=k, pb=pb, hi_=hi_, m=m, w=w: nc.tensor.matmul(pb.t[:, :], w.t[:, k, m * 128:(m + 1) * 128], inb[hi_].t[:, k, :],
                                                                                               start=(k == 0), stop=(k == kc - 1))) for k in range(kc)],
                                       reads=[w, inb[hi_]], writes=[pb])
                            job['epi'](c0 + m * 128, t0, pb)
                else:
                    for hi_, t0 in act:
                        for sub in range(4):
                            pb = self.bank(0, 4)
                            S.mm_group([(lambda k=k, pb=pb, hi_=hi_, sub=sub, w=w: nc.tensor.matmul(pb.t[:, 0:ncols], inb[hi_].t[:, k, sub * 128:(sub + 1) * 128], w.t[:, k, 0:ncols],
                                                                                                   start=(k == 0), stop=(k == kc - 1))) for k in range(kc)],
                                       reads=[w, inb[hi_]], writes=[pb])
                            job['epi'](c0, t0 + sub * 128, pb)
        S.pop()

    def rope_tables(self, pos_ap):
        S, nc = self.S, self.nc
        T = self.T
        self.cosT = S.sb("cosT", [32, T], F32)
        self.ssinT = S.sb("ssinT", [32, T], F32)
        S.push()
        pi_ = S.sb("pos_i", [32, T], I32)
        ang = S.sb("ang", [32, T], F32)
        tmp = S.sb("atmp", [32, T], F32)
        ki = S.sb("aki", [32, T], I32)
        kf = S.sb("akf", [32, T], F32)
        S.load('sp', pi_, pi_.t[:], pos_ap[0:1, :].to_broadcast([32, T]))
        S.op('dve', lambda: nc.vector.tensor_copy(ang.t[:], pi_.t[:]), reads=[pi_], writes=[ang])
        S.op('dve', lambda: nc.vector.tensor_scalar(ang.t[:], ang.t[:], self.cst.t[0:32, 129:130], None, op0=ALU.mult), reads=[ang, self.cst], writes=[ang])
        for which, off, dst in (("sin", 0.0, self.ssinT), ("cos", TWO_PI / 4, self.cosT)):
            S.op('dve', lambda off=off: nc.vector.tensor_scalar(tmp.t[:], ang.t[:], off, None, op0=ALU.add), reads=[ang], writes=[tmp])
            S.op('dve', lambda: nc.vector.tensor_scalar(ki.t[:], tmp.t[:], 1.0 / TWO_PI, None, op0=ALU.mult), reads=[tmp], writes=[ki])
            S.op('dve', lambda: nc.vector.tensor_copy(kf.t[:], ki.t[:]), reads=[ki], writes=[kf])
            S.op('dve', lambda: nc.vector.scalar_tensor_tensor(tmp.t[:], kf.t[:], -TWO_PI, tmp.t[:], op0=ALU.mult, op1=ALU.add), reads=[kf, tmp], writes=[tmp])
            S.op('dve', lambda: nc.vector.tensor_scalar(tmp.t[:], tmp.t[:], 3.14159, -3.14159, op0=ALU.min, op1=ALU.max), reads=[tmp], writes=[tmp])
            S.op('act', lambda dst=dst: nc.scalar.activation(dst.t[:], tmp.t[:], AF.Sin), reads=[tmp], writes=[dst])
        S.op('dve', lambda: nc.vector.tensor_scalar(self.ssinT.t[:], self.ssinT.t[:], self.cst.t[0:32, 128:129], None, op0=ALU.mult),
             reads=[self.ssinT, self.cst], writes=[self.ssinT])
        S.pop()

    def layer(self, L, x_ext, x_res, x_out, xo_res, W, cvec, seg_of, N=None, H=None):
        S, nc = self.S, self.nc
        if N is not None:
            self.N, self.H, self.T = N, H, N + H
        N, H, T = self.N, self.H, self.T
        pfx = "L%d_" % L
        modrow = self.phase_mod(L, cvec, W['w_ada'], W['b_ada'])
        mt = self.load_modT(L, modrow)
        mr = self.R("modrow%d" % L)
        hT = self.phase_hT(pfx + "hT", x_ext, x_res, T, mt, 0, 1)
        hr = self.R(pfx + "hT")
        if getattr(self, 'stop_after', None) == 'P1':
            return
        all_tiles = list(range(0, T, 512))
        main = lambda t0: t0 >= H
        w_in = W['w_in'].rearrange("(p c) n -> p c n", c=KC)

        uT = self.scratch(pfx + "uT", [128, 8, T], BF16)
        qkT = self.scratch(pfx + "qkT", [128, 48, T], BF16)
        cvT = self.scratch(pfx + "cvT", [128, 24, T], F32)
        vg = self.scratch(pfx + "vg", [T, MIXW], F32)
        Vt = self.scratch(pfx + "Vt", [T, 3 * MIXW], BF16)
        brT = self.scratch(pfx + "brT", [128, 24, T], BF16)
        mgT = self.scratch(pfx + "mgT", [128, KC, T], BF16)
        rbuf = self.scratch(pfx + "rbuf", [N, D], F32)
        x1 = self.scratch(pfx + "x1", [N, D], F32)

        S.push()
        self.rope_tables(W['pos'])
        stg_b = [S.sb("stgb%d" % i, [128, 512], BF16) for i in range(3)]
        stg_f = [S.sb("stgf%d" % i, [128, 512], F32) for i in range(3)]
        r32 = [S.sb("r32_%d" % i, [32, 512], F32) for i in range(2)]
        t32 = [S.sb("t32_%d" % i, [32, 512], F32) for i in range(2)]
        permf = self.cst
        cnt = [0]

        def epi_u(c0, t0, pb):
            o = stg_b[cnt[0] % 3]; cnt[0] += 1
            S.op('act', lambda: nc.scalar.activation(o.t[:], pb.t[:], AF.Gelu_apprx_tanh), reads=[pb], writes=[o])
            S.store('sp', self.R(pfx + "uT"), uT[:, c0 // 128, t0:t0 + 512], o, o.t[:])

        def epi_qk(c0, t0, pb):
            i = cnt[0]; cnt[0] += 1
            o = stg_b[i % 3]; r = stg_f[i % 3]; tt = t32[i % 2]; rc = r32[i % 2]
            ch = (c0 - 2048) // 128
            S.op('act', lambda: nc.scalar.copy(r.t[:], pb.t[:]), reads=[pb], writes=[r])
            p2 = self.bank(4, 8)
            S.mm_group([lambda: nc.tensor.matmul(p2.t[:, :], self.cst.t[:, 0:128], r.t[:], start=True, stop=True)], reads=[self.cst, r], writes=[p2])
            S.op('dve', lambda: nc.vector.tensor_tensor(tt.t[:], p2.t[0:32, :], self.ssinT.t[:, t0:t0 + 512], op=ALU.mult), reads=[p2, self.ssinT], writes=[tt])
            S.op('pool', lambda: nc.gpsimd.tensor_tensor(rc.t[:], r.t[0:32, :], self.cosT.t[:, t0:t0 + 512], op=ALU.mult), reads=[r, self.cosT], writes=[rc])
            S.op('pool', lambda: nc.gpsimd.tensor_copy(o.t[:, :], r.t[:, :]), reads=[r], writes=[o])
            S.op('dve', lambda: nc.vector.tensor_tensor(o.t[0:32, :], rc.t[:], tt.t[:], op=ALU.add), reads=[rc, tt, o], writes=[o])
            S.store('sp', self.R(pfx + "qkT"), qkT[:, ch, t0:t0 + 512], o, o.t[:])

        def epi_cv(c0, t0, pb):
            o = stg_f[cnt[0] % 3]; cnt[0] += 1
            S.op('act', lambda: nc.scalar.copy(o.t[:], pb.t[:]), reads=[pb], writes=[o])
            S.store('sp', self.R(pfx + "cvT"), cvT[:, (c0 - 11264) // 128, t0:t0 + 512], o, o.t[:])

        jobs = []
        for c0 in range(0, 1024, 256):
            jobs.append(dict(c0=c0, ncols=256, want=main, epi=epi_u))
        for c0 in range(2048, 2048 + 3072, 256):
            jobs.append(dict(c0=c0, ncols=256, want=main, epi=epi_qk))
        for c0 in range(2048 + 3072, 2048 + 6144, 256):
            jobs.append(dict(c0=c0, ncols=256, want=lambda t0: True, epi=epi_qk))
        for c0 in range(11264, 11264 + 3072, 256):
            jobs.append(dict(c0=c0, ncols=256, want=lambda t0: t0 >= H - 512, epi=epi_cv))
        import os
        jf = os.environ.get('KDBG_JOBS')
        if jf:
            jobs = [j for j in jobs if j['epi'].__name__ in jf.split(',')]
        self.gemm('B', hT, hr, KC, all_tiles, w_in, jobs, wcols=256)
        S.pop()
        if getattr(self, 'stop_after', None) == 'P2':
            return

        S.push()
        stg_b = [S.sb("stgb%d" % i, [128, 512], BF16) for i in range(3)]
        stg_f = [S.sb("stgf%d" % i, [128, 512], F32) for i in range(3)]

        def epi_vg(c0, t0, pb):
            o = stg_f[cnt[0] % 3]; cnt[0] += 1
            S.op('act', lambda: nc.scalar.activation(o.t[:], pb.t[:], AF.Gelu_apprx_tanh), reads=[pb], writes=[o])
            S.store('sp', self.R(pfx + "vg"), vg[t0:t0 + 128, c0 - 1024:c0 - 1024 + 512], o, o.t[:])

        def epi_V(c0, t0, pb):
            o = stg_b[cnt[0] % 3]; cnt[0] += 1
            S.op('act', lambda: nc.scalar.copy(o.t[:], pb.t[:]), reads=[pb], writes=[o])
            S.store('sp', self.R(pfx + "Vt"), Vt[t0:t0 + 128, c0 - 8192:c0 - 8192 + 512], o, o.t[:])

        jobs = []
        for c0 in range(1024, 2048, 512):
            jobs.append(dict(c0=c0, ncols=512, want=main, epi=epi_vg))
        for c0 in range(8192, 8192 + 3072, 512):
            jobs.append(dict(c0=c0, ncols=512, want=lambda t0: True, epi=epi_V))
        self.gemm('A', hT, hr, KC, all_tiles, w_in, jobs)
        S.pop()

        if getattr(self, 'stop_after', None) == 'P3':
            return
        S.push()
        lng = S.sb("lng", [128, MIXW], F32)
        lnb = S.sb("lnb", [128, MIXW], F32)
        S.load('sp', lng, lng.t[:], W['ln_v_g'][0:1, :].to_broadcast([128, MIXW]))
        S.load('sp', lnb, lnb.t[:], W['ln_v_b'][0:1, :].to_broadcast([128, MIXW]))
        wspf = S.sb("wspf", [128, 8, 128], F32)
        wspb = S.sb("wspb", [128, 8, 128], BF16)
        bsp = S.sb("bsp", [1, 8, 128], F32)
        S.load('sp', wspf, wspf.t[:], W['w_spT'].rearrange("g s t -> s g t"))
        S.load('sp', bsp, bsp.t[:], W['b_sp'].rearrange("(o g) t -> o g t", o=1))
        for g in range(8):
            S.op('dve', lambda g=g: nc.vector.tensor_tensor(wspb.t[:, g, :], wspf.t[:, g, :], self.tri_le.t[:], op=ALU.mult), reads=[wspf, self.tri_le], writes=[wspb])
        vtb = [S.sb("vt%d" % i, [128, MIXW], F32) for i in range(2)]
        vnb = [S.sb("vn%d" % i, [128, MIXW], BF16) for i in range(2)]
        vtmp = S.sb("vtmp", [128, MIXW], F32)
        utb = [S.sb("ut%d" % i, [128, 8, 128], BF16) for i in range(2)]
        brb = [S.sb("bra%d" % i, [128, 8, 128], BF16) for i in range(2)]
        stats = S.sb("bnst", [128, 2, 6], F32)
        mv = S.sb("bnmv", [128, 2], F32)
        rstd = S.sb("rstd", [128, 1], F32)
        for ci in range(N // 128):
            e0 = H + ci * 128
            vt, vn, ut, br = vtb[ci % 2], vnb[ci % 2], utb[ci % 2], brb[ci % 2]
            S.load('sp', vt, vt.t[:], vg[e0:e0 + 128, :], reads=[self.R(pfx + "vg")])
            S.load('act', ut, ut.t[:], uT[:, :, e0:e0 + 128], reads=[self.R(pfx + "uT")])
            for hh in range(2):
                S.op('dve', lambda hh=hh: nc.vector.bn_stats(stats.t[:, hh, :], vt.t[:, hh * 512:(hh + 1) * 512]), reads=[vt], writes=[stats])
            S.op('dve', lambda: nc.vector.bn_aggr(mv.t[:], stats.t[:].rearrange("p a b -> p (a b)")), reads=[stats], writes=[mv])
            S.op('act', lambda: nc.scalar.activation(rstd.t[:], mv.t[:, 1:2], AF.Sqrt, bias=self.epsc.t[:], scale=1.0), reads=[mv, self.epsc], writes=[rstd])
            S.op('dve', lambda: nc.vector.reciprocal(rstd.t[:], rstd.t[:]), reads=[rstd], writes=[rstd])
            S.op('dve', lambda: nc.vector.tensor_scalar(vtmp.t[:], vt.t[:], mv.t[:, 0:1], rstd.t[:], op0=ALU.subtract, op1=ALU.mult), reads=[vt, mv, rstd], writes=[vtmp])
            S.op('pool', lambda: nc.gpsimd.tensor_tensor(vtmp.t[:], vtmp.t[:], lng.t[:], op=ALU.mult), reads=[vtmp, lng], writes=[vtmp])
            S.op('dve', lambda: nc.vector.tensor_tensor(vn.t[:], vtmp.t[:], lnb.t[:], op=ALU.add), reads=[vtmp, lnb], writes=[vn])
            for half in range(2):
                pb = self.bank()
                for gg in range(4):
                    g = half * 4 + gg
                    S.mm_group([lambda g=g, gg=gg, pb=pb: nc.tensor.matmul(pb.t[:, gg * 128:(gg + 1) * 128], vn.t[:, g * 128:(g + 1) * 128], wspb.t[:, g, :], start=True, stop=False),
                                lambda g=g, gg=gg, pb=pb: nc.tensor.matmul(pb.t[:, gg * 128:(gg + 1) * 128], self.ones_f.t[0:1, :], bsp.t[0:1, g, :], start=False, stop=True)],
                               reads=[vn, wspb, self.ones_f, bsp], writes=[pb])
                S.op('dve', lambda half=half, pb=pb: nc.vector.tensor_tensor(br.t[:, half * 4:(half + 1) * 4, :], pb.t[:, :].rearrange("p (g t) -> p g t", g=4),
                                                                            ut.t[:, half * 4:(half + 1) * 4, :], op=ALU.mult), reads=[pb, ut], writes=[br])
            S.store('sp', self.R(pfx + "brT"), brT[:, 0:8, e0:e0 + 128], br, br.t[:])
        S.pop()

        if getattr(self, 'stop_after', None) == 'P4':
            return
        S.push()
        kTb = [S.sb("kTb%d" % i, [128, T], BF16) for i in range(2)]
        qTb = [S.sb("qTb%d" % i, [128, N], BF16) for i in range(2)]
        Vhb = [S.sb("Vhb%d" % i, [128, T // 128, 128], BF16) for i in range(2)]
        acc = S.sb("accOL", [128, 2, N], F32)
        Etb = [S.sb("Et%d" % i, [128, 2, 128], BF16) for i in range(4)]
        Emb = [S.sb("Em%d" % i, [128, 2, 128], BF16) for i in range(4)]
        rec = S.sb("recL", [128, N], F32)
        ao = S.sb("attn_o", [128, N], BF16)
        it = 0
        for h in range(8):
            for g, dil in enumerate((1, 4, 16)):
                hh = g * 8 + h
                kT, qT, Vh = kTb[(h * 3 + g) % 2], qTb[(h * 3 + g) % 2], Vhb[(h * 3 + g) % 2]
                S.load('sp', kT, kT.t[:], qkT[:, 24 + hh, :], reads=[self.R(pfx + "qkT")])
                S.load('act', qT, qT.t[:], qkT[:, hh, H:H + N], reads=[self.R(pfx + "qkT")])
                nT = T // (128 * dil)
                vsrc = Vt[:, hh * 128:(hh + 1) * 128].rearrange("(T k r) d -> r k T d", k=128, r=dil)
                for r in range(dil):
                    S.load('sp' if r % 2 == 0 else 'act', Vh, Vh.t[:, r * nT:(r + 1) * nT, :], vsrc[r], reads=[self.R(pfx + "Vt")])
                for r in range(dil):
                    for Tq in range(H // (128 * dil), nT):
                        Et, Em = Etb[it % 4], Emb[it % 4]
                        it += 1
                        q0 = Tq * 128 * dil + r - H
                        qs = qT.t[:, q0:q0 + 127 * dil + 1:dil]
                        pb = self.bank(0, 4)
                        fns = []
                        for j, Tk in enumerate((Tq - 1, Tq)):
                            k0 = Tk * 128 * dil + r
                            ks = kT.t[:, k0:k0 + 127 * dil + 1:dil]
                            fns.append(lambda j=j, ks=ks, pb=pb: nc.tensor.matmul(pb.t[:, j * 128:(j + 1) * 128], ks, qs, start=True, stop=True))
                        S.mm_group(fns, reads=[kT, qT], writes=[pb])
                        S.op('act', lambda pb=pb, Et=Et: nc.scalar.activation(Et.t[:].rearrange("p a b -> p (a b)"), pb.t[:, 0:256], AF.Exp, scale=128.0 ** -0.5),
                             reads=[pb], writes=[Et])
                        seg = seg_of((Tq - 1) * 128 * dil + r)
                        mp = self.maskpair[seg]
                        S.op('pool', lambda Et=Et, Em=Em, mp=mp: nc.gpsimd.tensor_tensor(Em.t[:], Et.t[:], mp.t[:], op=ALU.mult), reads=[Et, mp], writes=[Em])
                        p2 = self.bank(4, 8)
                        fns = []
                        for j, Tk in enumerate((Tq - 1, Tq)):
                            fns.append(lambda j=j, Tk=Tk, p2=p2, Em=Em: nc.tensor.matmul(p2.t[:, 0:128], Vh.t[:, r * nT + Tk, :], Em.t[:, j, :], start=(j == 0), stop=(j == 1)))
                        S.mm_group(fns, reads=[Vh, Em], writes=[p2])
                        fns = []
                        for j in range(2):
                            fns.append(lambda j=j, p2=p2, Em=Em: nc.tensor.matmul(p2.t[:, 128:256], self.ones_b.t[:], Em.t[:, j, :], start=(j == 0), stop=(j == 1)))
                        S.mm_group(fns, reads=[self.ones_b, Em], writes=[p2])
                        dst = acc.t[:, :, q0:q0 + 127 * dil + 1:dil]
                        src = p2.t[:, 0:256].rearrange("p (a b) -> p a b", a=2)
                        if g == 0:
                            S.op('act', lambda dst=dst, src=src: nc.scalar.copy(dst, src), reads=[p2], writes=[acc])
                        else:
                            S.op('dve', lambda dst=dst, src=src: nc.vector.tensor_tensor(dst, dst, src, op=ALU.add), reads=[p2, acc], writes=[acc])
            S.op('dve', lambda: nc.vector.reciprocal(rec.t[:], acc.t[:, 1, :]), reads=[acc], writes=[rec])
            S.op('dve', lambda: nc.vector.tensor_tensor(ao.t[:], acc.t[:, 0, :], rec.t[:], op=ALU.mult), reads=[acc, rec], writes=[ao])
            S.store('sp', self.R(pfx + "brT"), brT[:, 8 + h, H:H + N], ao, ao.t[:])
        S.pop()

        if getattr(self, 'stop_after', None) == 'P5':
            return
        S.push()
        cw = S.sb("convw", [128, 8, 3], F32)
        S.load('sp', cw, cw.t[:], W['conv_w'][:, :, :])
        gcb = [S.sb("gc%d" % i, [128, 514], F32) for i in range(2)]
        xib = [S.sb("xi%d" % i, [128, 514], F32) for i in range(2)]
        gbb = [S.sb("gb%d" % i, [128, 512], F32) for i in range(2)]
        yb = [S.sb("cy%d" % i, [128, 512], F32) for i in range(2)]
        ob = [S.sb("co%d" % i, [128, 512], BF16) for i in range(2)]
        it = 0
        cr = self.R(pfx + "cvT")
        for j in range(8):
            for t0 in range(H, T, 512):
                gc, xi, gb, y, o = gcb[it % 2], xib[it % 2], gbb[it % 2], yb[it % 2], ob[it % 2]
                it += 1
                S.load('sp', gc, gc.t[:], cvT[:, 8 + j, t0 - 2:t0 + 512], reads=[cr])
                S.load('act', xi, xi.t[:], cvT[:, 16 + j, t0 - 2:t0 + 512], reads=[cr])
                S.load('sp', gb, gb.t[:], cvT[:, j, t0:t0 + 512], reads=[cr])
                S.op('pool', lambda: nc.gpsimd.tensor_tensor(gc.t[:], gc.t[:], xi.t[:], op=ALU.mult), reads=[gc, xi], writes=[gc])
                sg = seg_of(t0 - 1)
                if sg != seg_of(t0):
                    S.op('pool', lambda sg=sg: nc.gpsimd.tensor_scalar(gc.t[:, 0:2], gc.t[:, 0:2], self.flg.t[:, sg:sg + 1], None, op0=ALU.mult), reads=[gc, self.flg], writes=[gc])
                S.op('dve', lambda: nc.vector.tensor_scalar(y.t[:], gc.t[:, 0:512], cw.t[:, j, 0:1], None, op0=ALU.mult), reads=[gc, cw], writes=[y])
                S.op('dve', lambda: nc.vector.scalar_tensor_tensor(y.t[:], gc.t[:, 1:513], cw.t[:, j, 1:2], y.t[:], op0=ALU.mult, op1=ALU.add), reads=[gc, cw, y], writes=[y])
                S.op('dve', lambda: nc.vector.scalar_tensor_tensor(y.t[:], gc.t[:, 2:514], cw.t[:, j, 2:3], y.t[:], op0=ALU.mult, op1=ALU.add), reads=[gc, cw, y], writes=[y])
                S.op('pool', lambda: nc.gpsimd.tensor_tensor(o.t[:], y.t[:], gb.t[:], op=ALU.mult), reads=[y, gb], writes=[o])
                S.store('sp', self.R(pfx + "brT"), brT[:, 16 + j, t0:t0 + 512], o, o.t[:])
        S.pop()

        if getattr(self, 'stop_after', None) == 'P6':
            return
        S.push()
        hb = [S.sb("mhT%d" % i, [128, KC, 512], BF16) for i in range(1)]
        bb = [S.sb("mbr%d" % i, [128, 24, 512], BF16) for i in range(1)]
        wgb = [S.sb("wg%d" % i, [128, KC, 512], BF16) for i in range(2)]
        wbb = [S.sb("wbr%d" % i, [128, 8, 512], BF16) for i in range(2)]
        bg = S.sb("bgate", [128, 3, KC], F32)
        S.load('sp', bg, bg.t[:], W['b_gate'][:, :, :])
        gtb = [S.sb("gate%d" % i, [128, 512], F32) for i in range(2)]
        mac = [S.sb("macc%d" % i, [128, 4, 512], F32) for i in range(2)]
        mob = [S.sb("mout%d" % i, [128, 4, 512], BF16) for i in range(2)]
        wgv = W['w_gate'].rearrange("g (p c) n -> g p c n", c=KC)
        wbv = W['w_branch'].rearrange("g (c p) n -> g p c n", p=128)
        wi = 0
        gi = 0
        for ti, t0 in enumerate(range(H, T, 512)):
            h_, b_ = hb[0], bb[0]
            S.load('sp', h_, h_.t[:], hT[:, :, t0:t0 + 512], reads=[hr])
            S.load('act', b_, b_.t[:], brT[:, :, t0:t0 + 512], reads=[self.R(pfx + "brT")])
            for ms in range(8):
                ma, mo = mac[ms % 2], mob[ms % 2]
                for g in range(3):
                    wg, wbr = wgb[wi % 2], wbb[wi % 2]
                    wi += 1
                    S.load('pool', wg, wg.t[:], wgv[g, :, :, ms * 512:(ms + 1) * 512])
                    S.load('pool', wbr, wbr.t[:], wbv[g, :, :, ms * 512:(ms + 1) * 512])
                    for m in range(4):
                        gt = gtb[gi % 2]
                        gi += 1
                        pb = self.bank(0, 4)
                        S.mm_group([(lambda k=k, pb=pb, wg=wg, m=m: nc.tensor.matmul(pb.t[:, :], wg.t[:, k, m * 128:(m + 1) * 128], h_.t[:, k, :], start=(k == 0), stop=(k == KC - 1)))
                                    for k in range(KC)], reads=[wg, h_], writes=[pb])
                        cidx = ms * 4 + m
                        S.op('act', lambda pb=pb, gt=gt, g=g, cidx=cidx: nc.scalar.activation(gt.t[:], pb.t[:], AF.Sigmoid, bias=bg.t[:, g, cidx:cidx + 1], scale=1.0),
                             reads=[pb, bg], writes=[gt])
                        p2 = self.bank(4, 8)
                        S.mm_group([(lambda k=k, p2=p2, wbr=wbr, m=m, g=g: nc.tensor.matmul(p2.t[:, :], wbr.t[:, k, m * 128:(m + 1) * 128], b_.t[:, g * 8 + k, :], start=(k == 0), stop=(k == 7)))
                                    for k in range(8)], reads=[wbr, b_], writes=[p2])
                        if g == 0:
                            S.op('dve', lambda p2=p2, gt=gt, ma=ma, m=m: nc.vector.tensor_tensor(ma.t[:, m, :], p2.t[:], gt.t[:], op=ALU.mult), reads=[p2, gt], writes=[ma])
                        else:
                            S.op('dve', lambda p2=p2, gt=gt, m=m: nc.vector.tensor_tensor(gt.t[:], p2.t[:], gt.t[:], op=ALU.mult), reads=[p2, gt], writes=[gt])
                            if g == 1:
                                S.op('pool', lambda gt=gt, ma=ma, m=m: nc.gpsimd.tensor_tensor(ma.t[:, m, :], ma.t[:, m, :], gt.t[:], op=ALU.add), reads=[gt, ma], writes=[ma])
                            else:
                                S.op('pool', lambda gt=gt, ma=ma, mo=mo, m=m: nc.gpsimd.tensor_tensor(mo.t[:, m, :], ma.t[:, m, :], gt.t[:], op=ALU.add), reads=[gt, ma], writes=[mo])
                S.store('sp', self.R(pfx + "mgT"), mgT[:, ms * 4:(ms + 1) * 4, t0:t0 + 512], mo, mo.t[:])
        S.pop()

        if getattr(self, 'stop_after', None) == 'P7':
            return
        def resid_gemm(inT, in_res, kcs, w3_of, tiles, toff, xsrc, xsrc_res, xoff, gslot, outs):
            S.push()
            gbc = S.sb("gbc", [128, D], F32)
            S.load('sp', gbc, gbc.t[:], modrow[0:1, gslot * D:(gslot + 1) * D].to_broadcast([128, D]), reads=[mr])
            xtb = [S.sb("xres%d" % i, [128, 512], F32) for i in range(3)]
            rtb = [S.sb("rres%d" % i, [128, 512], F32) for i in range(3)]
            c2 = [0]
            for part, (out_ap, out_name) in enumerate(outs):
                def epi(c0, t0, pb, part=part, out_ap=out_ap, out_name=out_name):
                    i = c2[0]; c2[0] += 1
                    xt, rt = xtb[i % 3], rtb[i % 3]
                    m0 = t0 - toff
                    if part == 0:
                        S.load('act', xt, xt.t[:], xsrc[xoff + m0:xoff + m0 + 128, c0:c0 + 512], reads=[xsrc_res] if xsrc_res is not None else [])
                    S.op('dve', lambda: nc.vector.tensor_tensor(rt.t[:], pb.t[:], gbc.t[:, c0:c0 + 512], op=ALU.mult), reads=[pb, gbc], writes=[rt])
                    if part == 0:
                        S.op('dve', lambda: nc.vector.scalar_tensor_tensor(rt.t[:], xt.t[:], ALPHA, rt.t[:], op0=ALU.mult, op1=ALU.add), reads=[xt, rt], writes=[rt])
                    S.store('sp', self.R(out_name), out_ap[m0:m0 + 128, c0:c0 + 512], rt, rt.t[:])
                jobs = [dict(c0=c0, ncols=512, want=lambda t0: True, epi=epi) for c0 in range(0, D, 512)]
                self.gemm('A', inT(part), in_res, kcs, tiles, w3_of(part), jobs)
            S.pop()

        resid_gemm(lambda part: mgT, self.R(pfx + "mgT"), KC, lambda part: W['w_o'].rearrange("(c p) n -> p c n", p=128),
                   [t for t in all_tiles if main(t)], H, x_ext, x_res, H, 2, [(rbuf, pfx + "rbuf")])

        def ln_phase(parts, g_ap, b_ap, out_ap, out_res):
            S.push()
            lg = S.sb("lng", [128, D], F32)
            lb = S.sb("lnb", [128, D], F32)
            S.load('sp', lg, lg.t[:], g_ap[0:1, :].to_broadcast([128, D]))
            S.load('sp', lb, lb.t[:], b_ap[0:1, :].to_broadcast([128, D]))
            rtb = [[S.sb("lnr%d_%d" % (p, i), [128, D], F32) for i in range(2 if len(parts) == 1 else 1)] for p in range(len(parts))]
            otb = [S.sb("lno%d" % i, [128, D], F32) for i in range(2)]
            st = S.sb("lnst", [128, 8, 6], F32)
            mv_ = S.sb("lnmv", [128, 2], F32)
            rs = S.sb("lnrs", [128, 1], F32)
            for ti in range(N // 128):
                rts = [rtb[p][ti % len(rtb[p])] for p in range(len(parts))]
                for p, (pap, pname) in enumerate(parts):
                    S.load('sp' if p % 2 == 0 else 'act', rts[p], rts[p].t[:], pap[ti * 128:(ti + 1) * 128, :], reads=[self.R(pname)])
                r0 = rts[0]
                for p in range(1, len(parts)):
                    S.op('pool' if p % 2 else 'dve', (lambda p=p: nc.gpsimd.tensor_tensor(r0.t[:], r0.t[:], rts[p].t[:], op=ALU.add)) if p % 2 else
                         (lambda p=p: nc.vector.tensor_tensor(r0.t[:], r0.t[:], rts[p].t[:], op=ALU.add)), reads=[r0, rts[p]], writes=[r0])
                o = otb[ti % 2]
                for c in range(8):
                    S.op('dve', lambda c=c: nc.vector.bn_stats(st.t[:, c, :], r0.t[:, c * 512:(c + 1) * 512]), reads=[r0], writes=[st])
                S.op('dve', lambda: nc.vector.bn_aggr(mv_.t[:], st.t[:].rearrange("p a b -> p (a b)")), reads=[st], writes=[mv_])
                S.op('act', lambda: nc.scalar.activation(rs.t[:], mv_.t[:, 1:2], AF.Sqrt, bias=self.epsc.t[:], scale=1.0), reads=[mv_, self.epsc], writes=[rs])
                S.op('dve', lambda: nc.vector.reciprocal(rs.t[:], rs.t[:]), reads=[rs], writes=[rs])
                S.op('dve', lambda: nc.vector.tensor_scalar(r0.t[:], r0.t[:], mv_.t[:, 0:1], rs.t[:], op0=ALU.subtract, op1=ALU.mult), reads=[r0, mv_, rs], writes=[r0])
                S.op('pool', lambda: nc.gpsimd.tensor_tensor(r0.t[:], r0.t[:], lg.t[:], op=ALU.mult), reads=[r0, lg], writes=[r0])
                S.op('dve', lambda: nc.vector.tensor_tensor(o.t[:], r0.t[:], lb.t[:], op=ALU.add), reads=[r0, lb], writes=[o])
                S.store('sp', out_res, out_ap[ti * 128:(ti + 1) * 128, :], o, o.t[:])
            S.pop()

        ln_phase([(rbuf, pfx + "rbuf")], W['ln1_g'], W['ln1_b'], x1, self.R(pfx + "x1"))

        if getattr(self, 'stop_after', None) == 'P9':
            return
        h2T = self.phase_hT(pfx + "h2T", x1, self.R(pfx + "x1"), N, mt, 3, 4)
        h2r = self.R(pfx + "h2T")
        ptiles = list(range(0, N, 512))
        qpT = self.scratch(pfx + "qpT", [128, 8, N], BF16)
        GT = self.scratch(pfx + "GT", [128, 128, N], BF16)
        cfT = self.scratch(pfx + "cfT", [128, 128, N], BF16)
        parts = [self.scratch(pfx + "yp%d" % i, [N, D], F32) for i in range(4)]

        S.push()
        stg_b = [S.sb("stgb%d" % i, [128, 512], BF16) for i in range(3)]

        def epi_q(c0, t0, pb):
            o = stg_b[cnt[0] % 3]; cnt[0] += 1
            S.op('act', lambda: nc.scalar.copy(o.t[:], pb.t[:]), reads=[pb], writes=[o])
            S.store('sp', self.R(pfx + "qpT"), qpT[:, c0 // 128, t0:t0 + 512], o, o.t[:])
        self.gemm('B', h2T, h2r, KC, ptiles, W['w_pq'].rearrange("(p c) n -> p c n", c=KC),
                  [dict(c0=c0, ncols=512, want=lambda t0: True, epi=epi_q) for c0 in (0, 512)])
        S.pop()

        S.push()
        kbf = S.sb("keysf", [128, 8, 256], F32)
        kbd = S.sb("keysbd", [128, 8, 256], BF16)
        S.op('pool', lambda: nc.gpsimd.memset(kbf.t[:], 0.0), writes=[kbf])
        S.load('sp', kbf, kbf.t[0:64, :, 0:128], W['keysT'][:, 0, :, :].rearrange("h d n -> d h n"), reads=[kbf])
        S.load('sp', kbf, kbf.t[64:128, :, 128:256], W['keysT'][:, 1, :, :].rearrange("h d n -> d h n"), reads=[kbf])
        S.op('dve', lambda: nc.vector.tensor_copy(kbd.t[:], kbf.t[:]), reads=[kbf], writes=[kbd])
        qtb = [S.sb("qpt%d" % i, [128, 8, 128], BF16) for i in range(2)]
        sc = S.sb("scores", [128, 8, 256], F32)
        v16 = S.sb("v16", [128, 16, 16], F32)
        scr = S.sb("mscr", [128, 256], F32)
        cand = S.sb("cand", [128, 256], F32)
        c16 = S.sb("c16", [128, 16], F32)
        e16 = S.sb("e16", [128, 16], F32)
        negm = S.sb("negm", [128, 16], F32)
        zz = S.sb("zz", [128, 8], F32)
        rz = S.sb("rz", [128, 8], F32)
        tau = S.sb("tau", [128, 8], F32)
        nmx = S.sb("nmx", [128, 8], F32)
        a1 = S.sb("a1", [128, 8, 128], F32)
        a2 = S.sb("a2", [128, 8, 128], F32)
        th = S.sb("theta", [128, 8, 128], F32)
        tmb = [S.sb("tmpg%d" % i, [128, 4, 128], F32) for i in range(4)]
        t2b = [S.sb("tmpb%d" % i, [128, 4, 128], BF16) for i in range(4)]
        gtt = [S.sb("GTt%d" % i, [128, 128, 128], BF16) for i in range(2)]
        gi = 0
        for ti in range(N // 128):
            qt = qtb[ti % 2]
            S.load('sp', qt, qt.t[:], qpT[:, :, ti * 128:(ti + 1) * 128], reads=[self.R(pfx + "qpT")])
            for hp in range(4):
                pb = self.bank()
                S.mm_group([(lambda j=j, pb=pb: nc.tensor.matmul(pb.t[:, j * 256:(j + 1) * 256], qt.t[:, hp * 2 + j, :], kbd.t[:, hp * 2 + j, :], start=True, stop=True))
                            for j in range(2)], reads=[qt, kbd], writes=[pb])
                S.op('act', lambda pb=pb, hp=hp: nc.scalar.copy(sc.t[:, hp * 2:hp * 2 + 2, :].rearrange("p a b -> p (a b)"), pb.t[:, :]), reads=[pb], writes=[sc])
            for h in range(8):
                for p in range(2):
                    sv = sc.t[:, h, p * 128:(p + 1) * 128]
                    S.op('dve', lambda sv=sv, h=h, p=p: nc.vector.max(v16.t[:, h * 2 + p, 0:8], sv), reads=[sc], writes=[v16])
                    S.op('dve', lambda sv=sv, h=h, p=p: nc.vector.match_replace(scr.t[:, 0:128], v16.t[:, h * 2 + p, 0:8], sv, NEG), reads=[sc, v16], writes=[scr])
                    S.op('dve', lambda h=h, p=p: nc.vector.max(v16.t[:, h * 2 + p, 8:16], scr.t[:, 0:128]), reads=[scr], writes=[v16])
                S.op('dve', lambda h=h: nc.vector.tensor_tensor(cand.t[:].rearrange("p (a b) -> p a b", a=16), v16.t[:, h * 2, :].unsqueeze(2).to_broadcast([128, 16, 16]),
                                                              v16.t[:, h * 2 + 1, :].unsqueeze(1).to_broadcast([128, 16, 16]), op=ALU.add), reads=[v16], writes=[cand])
                S.op('dve', lambda: nc.vector.max(c16.t[:, 0:8], cand.t[:]), reads=[cand], writes=[c16])
                S.op('dve', lambda: nc.vector.match_replace(scr.t[:], c16.t[:, 0:8], cand.t[:], NEG), reads=[cand, c16], writes=[scr])
                S.op('dve', lambda: nc.vector.max(c16.t[:, 8:16], scr.t[:]), reads=[scr], writes=[c16])
                S.op('dve', lambda h=h: nc.vector.tensor_scalar(tau.t[:, h:h + 1], c16.t[:, 15:16], -2.0e-5, None, op0=ALU.add), reads=[c16], writes=[tau])
                S.op('dve', lambda h=h: nc.vector.tensor_scalar(nmx.t[:, h:h + 1], c16.t[:, 0:1], -1.0, None, op0=ALU.mult), reads=[c16], writes=[nmx])
                S.op('act', lambda h=h: nc.scalar.activation(e16.t[:], c16.t[:], AF.Exp, bias=nmx.t[:, h:h + 1], scale=1.0, accum_out=zz.t[:, h:h + 1]),
                     reads=[c16, nmx], writes=[e16, zz])
            S.op('dve', lambda: nc.vector.reciprocal(rz.t[:], zz.t[:]), reads=[zz], writes=[rz])
            S.op('dve', lambda: nc.vector.tensor_scalar(negm.t[:], v16.t[:, :, 0], -1.0, None, op0=ALU.mult), reads=[v16], writes=[negm])
            for h in range(8):
                S.op('act', lambda h=h: nc.scalar.activation(a1.t[:, h, :], sc.t[:, h, 0:128], AF.Exp, bias=negm.t[:, 2 * h:2 * h + 1], scale=1.0), reads=[sc, negm], writes=[a1])
                S.op('act', lambda h=h: nc.scalar.activation(a2.t[:, h, :], sc.t[:, h, 128:256], AF.Exp, bias=negm.t[:, 2 * h + 1:2 * h + 2], scale=1.0), reads=[sc, negm], writes=[a2])
                S.op('dve', lambda h=h: nc.vector.tensor_scalar(a1.t[:, h, :], a1.t[:, h, :], rz.t[:, h:h + 1], None, op0=ALU.mult), reads=[a1, rz], writes=[a1])
                S.op('dve', lambda h=h: nc.vector.tensor_scalar(th.t[:, h, :], sc.t[:, h, 0:128], -1.0, tau.t[:, h:h + 1], op0=ALU.mult, op1=ALU.add), reads=[sc, tau], writes=[th])
            gt_ = gtt[ti % 2]
            for ig in range(32):
                pb = self.bank()
                fns = []
                used = []
                for h in range(8):
                    tm, t2 = tmb[gi % 4], t2b[gi % 4]
                    gi += 1
                    for ii in range(4):
                        i = ig * 4 + ii
                        if h < 6:
                            S.op('dve', lambda h=h, i=i, ii=ii, tm=tm: nc.vector.scalar_tensor_tensor(tm.t[:, ii, :], sc.t[:, h, 128:256], th.t[:, h, i:i + 1], a2.t[:, h, :],
                                                                                                     op0=ALU.is_ge, op1=ALU.mult), reads=[sc, th, a2], writes=[tm])
                        else:
                            S.op('pool', lambda h=h, i=i, ii=ii, tm=tm: nc.gpsimd.tensor_scalar(tm.t[:, ii, :], sc.t[:, h, 128:256], th.t[:, h, i:i + 1], None, op0=ALU.is_ge),
                                 reads=[sc, th], writes=[tm])
                            S.op('pool', lambda h=h, ii=ii, tm=tm: nc.gpsimd.tensor_tensor(tm.t[:, ii, :], tm.t[:, ii, :], a2.t[:, h, :], op=ALU.mult), reads=[tm, a2], writes=[tm])
                    for ii in range(4):
                        i = ig * 4 + ii
                        S.op('act', lambda h=h, i=i, ii=ii, tm=tm, t2=t2: nc.scalar.activation(t2.t[:, ii, :], tm.t[:, ii, :], AF.Identity, scale=a1.t[:, h, i:i + 1]),
                             reads=[tm, a1], writes=[t2])
                    S.mm_group([(lambda ii=ii, t2=t2, pb=pb, h=h: nc.tensor.matmul(pb.t[:, ii * 128:(ii + 1) * 128], t2.t[:, ii, :], self.ident_b.t[:], start=(h == 0 and ii == 0), stop=(h == 7 and ii == 3),
                                                                                  skip_group_check=True)) for ii in range(4)], reads=[t2, self.ident_b], writes=[pb])
                S.op('act', lambda pb=pb, ig=ig: nc.scalar.copy(gt_.t[:, ig * 4:(ig + 1) * 4, :].rearrange("p a b -> p (a b)"), pb.t[:, :]), reads=[pb], writes=[gt_])
            S.store('sp', self.R(pfx + "GT"), GT[:, :, ti * 128:(ti + 1) * 128], gt_, gt_.t[:])
        S.pop()

        if getattr(self, 'stop_after', None) == 'Q2':
            return
        S.push()
        stg_f = [S.sb("stgf%d" % i, [128, 512], F32) for i in range(3)]
        stg_b = [S.sb("stgb%d" % i, [128, 512], BF16) for i in range(3)]
        gld = [S.sb("gld%d" % i, [128, 512], BF16) for i in range(3)]

        def epi_e(c0, t0, pb):
            i = cnt[0]; cnt[0] += 1
            a, o, gl = stg_f[i % 3], stg_b[i % 3], gld[i % 3]
            S.load('act', gl, gl.t[:], GT[:, c0 // 128, t0:t0 + 512], reads=[self.R(pfx + "GT")])
            S.op('act', lambda: nc.scalar.activation(a.t[:], pb.t[:], AF.Gelu_apprx_tanh), reads=[pb], writes=[a])
            S.op('dve', lambda: nc.vector.tensor_tensor(o.t[:], a.t[:], gl.t[:], op=ALU.mult), reads=[a, gl], writes=[o])
            S.store('sp', self.R(pfx + "cfT"), cfT[:, c0 // 128, t0:t0 + 512], o, o.t[:])
        self.gemm('B', h2T, h2r, KC, ptiles, W['w_uT'].rearrange("(p c) n -> p c n", c=KC),
                  [dict(c0=c0, ncols=512, want=lambda t0: True, epi=epi_e) for c0 in range(0, 16384, 512)])
        S.pop()

        if getattr(self, 'stop_after', None) == 'Q3':
            return
        wvv = W['w_v'].rearrange("(c p) n -> p c n", p=128)
        resid_gemm(lambda part: cfT[:, part * 32:(part + 1) * 32, :], self.R(pfx + "cfT"), 32, lambda part: wvv[:, part * 32:(part + 1) * 32, :],
                   ptiles, 0, x1, self.R(pfx + "x1"), 0, 5, [(parts[i], pfx + "yp%d" % i) for i in range(4)])
        ln_phase([(parts[i], pfx + "yp%d" % i) for i in range(4)], W['ln2_g'], W['ln2_b'], x_out, xo_res)


def _consts():
    c = np.zeros((128, 256), np.float32)
    for m in range(32):
        c[(m + 16) % 32, m] = 1.0
    c[0:16, 128] = -1.0
    c[16:32, 128] = 1.0
    half = 16
    inv = (np.float32(500000.0) ** (-np.arange(half, dtype=np.float32) / np.float32(half))).astype(np.float32)
    c[0:16, 129] = inv
    c[16:32, 129] = inv
    return c


WNAMES = ['w_ada', 'b_ada', 'w_in', 'w_gate', 'b_gate', 'ln_v_g', 'ln_v_b', 'w_spT', 'b_sp', 'conv_w', 'w_branch', 'w_o',
          'ln1_g', 'ln1_b', 'w_pq', 'keysT', 'w_uT', 'w_v', 'ln2_g', 'ln2_b']
WSHAPES = {'w_ada': [D, 6 * D], 'b_ada': [1, 6 * D], 'w_in': [D, 14336], 'w_gate': [3, D, D], 'b_gate': [128, 3, KC],
           'ln_v_g': [1, MIXW], 'ln_v_b': [1, MIXW], 'w_spT': [8, 128, 128], 'b_sp': [8, 128], 'conv_w': [128, 8, 3],
           'w_branch': [3, MIXW, D], 'w_o': [D, D], 'ln1_g': [1, D], 'ln1_b': [1, D], 'w_pq': [D, 1024],
           'keysT': [8, 2, 64, 128], 'w_uT': [D, 16384], 'w_v': [16384, D], 'ln2_g': [1, D], 'ln2_b': [1, D]}


def build_single_layer(debug=(), stop_after=None):
    N, H = CH, CH
    B = Builder(N, H, 1, debug=debug)
    B.stop_after = stop_after
    x_ext = B.din("x_ext", [N + H, D])
    cvec = B.din("cvec", [D])
    pos = B.din("pos", [1, N + H], I32)
    consts = B.din("consts", [128, 256])
    flags = B.din("flags", [128, 4])
    class LazyW(dict):
        def __missing__(self, n):
            self[n] = B.din(n, WSHAPES[n])
            return self[n]
    W = LazyW()
    W['pos'] = pos
    if stop_after is None:
        for n in WNAMES:
            W[n]
    x_out = B.dout("x_out", [N, D])
    B.setup_consts(consts, flags)
    B.layer(0, x_ext, None, x_out, B.R("x_out"), W, cvec, lambda e: e // CH)
    B.S.finish()
    return B


def host_layer_weights(inp, l):
    f = lambda a: np.ascontiguousarray(np.asarray(a, dtype=np.float32))
    return {
        'w_ada': f(inp['w_ada'][l]), 'b_ada': f(inp['b_ada'][l]).reshape(1, -1), 'w_in': f(inp['w_in'][l]),
        'w_gate': f(inp['w_gate'][l]), 'b_gate': f(np.transpose(np.asarray(inp['b_gate'][l]).reshape(3, KC, 128), (2, 0, 1))),
        'ln_v_g': f(inp['ln_v_g'][l]).reshape(1, -1), 'ln_v_b': f(inp['ln_v_b'][l]).reshape(1, -1),
        'w_spT': f(np.transpose(np.asarray(inp['w_sp'][l]), (0, 2, 1))), 'b_sp': f(inp['b_sp'][l]),
        'conv_w': f(np.transpose(np.asarray(inp['conv_w'][l]).reshape(3, 8, 128), (2, 1, 0))), 'w_branch': f(inp['w_branch'][l]), 'w_o': f(inp['w_o'][l]),
        'ln1_g': f(inp['ln1_g'][l]).reshape(1, -1), 'ln1_b': f(inp['ln1_b'][l]).reshape(1, -1),
        'w_pq': f(inp['w_pq'][l]), 'keysT': f(np.transpose(np.asarray(inp['sub_keys'][l]), (0, 1, 3, 2))),
        'w_uT': f(np.asarray(inp['w_u'][l]).T), 'w_v': f(inp['w_v'][l]),
        'ln2_g': f(inp['ln2_g'][l]).reshape(1, -1), 'ln2_b': f(inp['ln2_b'][l]).reshape(1, -1),
    }


def core_inputs(xfull, cfull, posfull, core):
    b, j = core // 4, core % 4
    main = xfull[b, j * CH:(j + 1) * CH]
    if j > 0:
        halo = xfull[b, (j - 1) * CH:j * CH]
        ph = posfull[b, (j - 1) * CH:j * CH]
    else:
        halo = np.zeros_like(main)
        ph = np.zeros(CH, np.int32)
    flags = np.zeros((128, 4), np.float32)
    flags[:, 0] = 1.0 if j > 0 else 0.0
    flags[:, 1:] = 1.0
    return {"x_ext": np.ascontiguousarray(np.concatenate([halo, main], 0)),
            "cvec": np.ascontiguousarray(cfull[b]),
            "pos": np.ascontiguousarray(np.concatenate([ph, posfull[b, j * CH:(j + 1) * CH]])[None, :].astype(np.int32)),
            "consts": _consts(), "flags": flags}


def build_fused():
    B = Builder(2 * CH, CH, 2)
    B.stop_after = None
    x_ext = B.din("x_ext", [3 * CH, D])
    cvec = B.din("cvec", [D])
    pos = B.din("pos", [1, 3 * CH], I32)
    consts = B.din("consts", [128, 256])
    flags = B.din("flags", [128, 4])
    Ws = []
    for l in range(DEPTH):
        W = {n: B.din("%s_%d" % (n, l), WSHAPES[n]) for n in WNAMES}
        Ws.append(W)
    x_out = B.dout("x_out", [CH, D])
    xmid = B.scratch("xmid", [2 * CH, D], F32)
    B.setup_consts(consts, flags)
    Ws[0]['pos'] = pos
    B.layer(0, x_ext, None, xmid, B.R("xmid"), Ws[0], cvec, lambda e: e // CH, N=2 * CH, H=CH)
    Ws[1]['pos'] = pos[:, CH:3 * CH]
    B.layer(1, xmid, B.R("xmid"), x_out, B.R("x_out"), Ws[1], cvec, lambda e: 1 + e // CH, N=CH, H=CH)
    B.S.finish()
    return B


def core_inputs_fused(xfull, cfull, posfull, core):
    b, j = core // 4, core % 4
    xs, ps = [], []
    for jj in (j - 2, j - 1, j):
        if jj >= 0:
            xs.append(xfull[b, jj * CH:(jj + 1) * CH]); ps.append(posfull[b, jj * CH:(jj + 1) * CH])
        else:
            xs.append(np.zeros((CH, D), np.float32)); ps.append(np.zeros(CH, np.int32))
    flags = np.ones((128, 4), np.float32)
    flags[:, 0] = 1.0 if j >= 2 else 0.0
    flags[:, 1] = 1.0 if j >= 1 else 0.0
    return {"x_ext": np.ascontiguousarray(np.concatenate(xs, 0)), "cvec": np.ascontiguousarray(cfull[b]),
            "pos": np.ascontiguousarray(np.concatenate(ps)[None, :].astype(np.int32)), "consts": _consts(), "flags": flags}


_PROG = {}


def kernel_unfused(**inputs):
    x = np.asarray(inputs['x'], dtype=np.float32)
    c = np.asarray(inputs['c'], dtype=np.float32)
    pos = np.asarray(inputs['positions']).astype(np.int32)
    if 'single' not in _PROG:
        _PROG['single'] = build_single_layer()
    B = _PROG['single']
    cur = x
    for l in range(DEPTH):
        Wl = host_layer_weights(inputs, l)
        in_maps = []
        for core in range(NCORES):
            m = core_inputs(cur, c, pos, core)
            m.update(Wl)
            in_maps.append(m)
        res = run_bass_kernel_spmd(B.nc, in_maps, core_ids=list(range(NCORES)))
        nxt = np.empty_like(x)
        for core in range(NCORES):
            b, j = core // 4, core % 4
            nxt[b, j * CH:(j + 1) * CH] = res.results[core]["x_out"]
        cur = nxt
    return cur


def kernel(**inputs):
    x = np.asarray(inputs['x'], dtype=np.float32)
    c = np.asarray(inputs['c'], dtype=np.float32)
    pos = np.asarray(inputs['positions']).astype(np.int32)
    if 'fused' not in _PROG:
        _PROG['fused'] = build_fused()
    B = _PROG['fused']
    Wall = {}
    for l in range(DEPTH):
        for n, v in host_layer_weights(inputs, l).items():
            Wall["%s_%d" % (n, l)] = v
    in_maps = []
    for core in range(NCORES):
        m = core_inputs_fused(x, c, pos, core)
        m.update(Wall)
        in_maps.append(m)
    res = run_bass_kernel_spmd(B.nc, in_maps, core_ids=list(range(NCORES)))
    out = np.empty_like(x)
    for core in range(NCORES):
        b, j = core // 4, core % 4
        out[b, j * CH:(j + 1) * CH] = res.results[core]["x_out"]
    return out
```

```python
import numpy as np
import concourse.bass as bass
import concourse.mybir as mybir
from concourse.bass_utils import run_bass_kernel_spmd

F32 = mybir.dt.float32
BF16 = mybir.dt.bfloat16
I32 = mybir.dt.int32
AF = mybir.ActivationFunctionType
ALU = mybir.AluOpType

D = 4096
KC = 32
SEQ = 8192
DEPTH = 2
MIXW = 1024
NCORES = 8
CH = 2048
ALPHA = (2 * DEPTH) ** 0.25
LN_EPS = 1e-5
NEG = -1.0e30
TWO_PI = 6.283185307179586


class Res:
    def __init__(self, name, t=None):
        self.name = name
        self.t = t
        self.w = {}
        self.r = {}
        self.lsem = None
        self.ssem = None


class Sched:
    def __init__(self, nc):
        self.nc = nc
        self.eng = {'pe': nc.tensor, 'act': nc.scalar, 'dve': nc.vector, 'pool': nc.gpsimd, 'sp': nc.sync}
        self.sems = {}
        self.cnt = {}
        self.seen = {e: {} for e in self.eng}
        self.ctxs = []
        self.marks = []
        self.free_dsem = []
        self.ndsem = 0
        self.n_inst = 0
        for e in self.eng:
            self._mksem(e)

    def _mksem(self, key):
        cm = self.nc.semaphore("s_" + key)
        h = cm.__enter__()
        self.sems[key] = h
        self.cnt[key] = 0
        return h

    def _dsem(self):
        if self.free_dsem:
            return self.free_dsem.pop()
        key = "d%d" % self.ndsem
        self.ndsem += 1
        self._mksem(key)
        return key

    def push(self):
        self.marks.append((len(self.ctxs), []))

    def pop(self):
        self.barrier()
        n, sems = self.marks.pop()
        while len(self.ctxs) > n:
            cm = self.ctxs.pop()
            cm.__exit__(None, None, None)
        self.free_dsem.extend(sems)

    def _alloc(self, cm, name):
        t = cm.__enter__()
        self.ctxs.append(cm)
        r = Res(name, t)
        r.sched = self
        return r

    def sb(self, name, shape, dt):
        self.uid = getattr(self, "uid", 0) + 1
        return self._alloc(self.nc.sbuf_tensor(name + "_u%d" % self.uid, shape, dt), name)

    def ps(self, name, shape, dt):
        return self._alloc(self.nc.psum_tensor(name, shape, dt), name)

    def _wait(self, e, deps, raw=None):
        eng = self.eng[e]
        seen = self.seen[e]
        for k, v in deps.items():
            if k == e:
                if e == 'pe' or raw is None or k not in raw:
                    continue
                v = raw[k]
            if seen.get(k, 0) >= v:
                continue
            eng.wait_ge(self.sems[k], v)
            self.n_inst += 1
            seen[k] = v

    @staticmethod
    def _add(deps, k, v):
        if deps.get(k, 0) < v:
            deps[k] = v

    def _raw(self, reads):
        raw = {}
        for r in reads:
            for k, v in r.w.items():
                self._add(raw, k, v)
        return raw

    def _deps(self, reads, writes):
        deps = {}
        for r in reads:
            for k, v in r.w.items():
                self._add(deps, k, v)
        for w in writes:
            for k, v in w.w.items():
                self._add(deps, k, v)
            for k, v in w.r.items():
                self._add(deps, k, v)
        return deps

    def _mark(self, k, v, reads, writes, acc=False):
        for r in reads:
            if r.r.get(k, 0) < v:
                r.r[k] = v
        for w in writes:
            if acc:
                if w.w.get(k, 0) < v:
                    w.w[k] = v
            else:
                w.w = {k: v}
                w.r = {}

    def op(self, e, fn, reads=(), writes=()):
        self._wait(e, self._deps(reads, writes), self._raw(reads))
        inst = fn()
        self.cnt[e] += 1
        inst.then_inc(self.sems[e], 1)
        self._mark(e, self.cnt[e], reads, writes)
        self.n_inst += 1
        return inst

    def mm_group(self, fns, reads, writes):
        self._wait('pe', self._deps(reads, writes))
        inst = None
        for fn in fns:
            inst = fn()
        self.cnt['pe'] += 1
        inst.then_inc(self.sems['pe'], 1)
        self._mark('pe', self.cnt['pe'], reads, writes)
        self.n_inst += len(fns)

    def load(self, q, dst, out_ap, in_ap, reads=(), **kw):
        self._wait(q, self._deps(reads, [dst]))
        if dst.lsem is None:
            dst.lsem = self._dsem()
            if self.marks:
                self.marks[-1][1].append(dst.lsem)
        k = dst.lsem
        inst = self.eng[q].dma_start(out=out_ap, in_=in_ap, **kw)
        self.cnt[k] += 16
        inst.then_inc(self.sems[k], 16)
        self._mark(k, self.cnt[k], reads, [dst])
        self.n_inst += 1

    def store(self, q, dst, out_ap, src, in_ap, extra=(), **kw):
        self._wait(q, self._deps([src] + list(extra), [dst]))
        if src.ssem is None:
            src.ssem = self._dsem()
            if self.marks:
                self.marks[-1][1].append(src.ssem)
        k = src.ssem
        inst = self.eng[q].dma_start(out=out_ap, in_=in_ap, **kw)
        self.cnt[k] += 16
        inst.then_inc(self.sems[k], 16)
        self._mark(k, self.cnt[k], [src] + list(extra), [dst], acc=True)
        self.n_inst += 1

    def barrier(self):
        deps = {k: v for k, v in self.cnt.items() if v > 0}
        for e in self.eng:
            self._wait(e, deps)

    def finish(self):
        self.barrier()
        while self.ctxs:
            self.ctxs.pop().__exit__(None, None, None)


class Builder:
    def __init__(self, N, H, nlayers, debug=()):
        self.N, self.H, self.T = N, H, N + H
        self.debug = set(debug)
        self.nc = bass.Bass("TRN2", target_bir_lowering=False)
        self.S = Sched(self.nc)
        self.dram_res = {}
        self.ndram = 0

    def din(self, name, shape, dt=F32):
        return self.nc.dram_tensor(name, list(shape), dt, kind="ExternalInput").ap()

    def dout(self, name, shape, dt=F32):
        ap = self.nc.dram_tensor(name, list(shape), dt, kind="ExternalOutput").ap()
        self.dram_res[name] = Res(name)
        return ap

    def scratch(self, name, shape, dt):
        kind = "ExternalOutput" if name in self.debug else "Internal"
        ap = self.nc.dram_tensor(name, list(shape), dt, kind=kind).ap()
        self.dram_res[name] = Res(name)
        return ap

    def R(self, name):
        return self.dram_res[name]

    def setup_consts(self, consts, flags):
        S, nc = self.S, self.nc
        self.ident_f = S.sb("ident_f", [128, 128], F32)
        self.ident_b = S.sb("ident_b", [128, 128], BF16)
        self.tri_ge = S.sb("tri_ge", [128, 128], F32)
        self.tri_le = S.sb("tri_le", [128, 128], F32)
        self.ones_b = S.sb("ones_b", [128, 128], BF16)
        self.ones_f = S.sb("ones_f", [1, 128], F32)
        self.cst = S.sb("cst", [128, 256], F32)
        self.flg = S.sb("flg", [128, 4], F32)
        self.epsc = S.sb("epsc", [128, 1], F32)
        S.load('sp', self.cst, self.cst.t[:], consts[:, :])
        S.load('sp', self.flg, self.flg.t[:], flags[:, :])
        S.op('pool', lambda: nc.gpsimd.memset(self.epsc.t[:], LN_EPS), writes=[self.epsc])
        S.op('pool', lambda: nc.gpsimd.memset(self.ones_b.t[:], 1.0), writes=[self.ones_b])
        S.op('pool', lambda: nc.gpsimd.memset(self.ones_f.t[:], 1.0), writes=[self.ones_f])
        for t, op, dt in ((self.ident_f, ALU.is_equal, None), (self.tri_ge, ALU.is_ge, None), (self.tri_le, ALU.is_ge, -1)):
            S.op('pool', lambda t=t: nc.gpsimd.memset(t.t[:], 1.0), writes=[t])
            if dt is None:
                S.op('pool', lambda t=t, op=op: nc.gpsimd.affine_select(out=t.t[:], in_=t.t[:], pattern=[[-1, 128]], compare_op=op,
                                                                    fill=0.0, base=0, channel_multiplier=1), reads=[t], writes=[t])
            else:
                S.op('pool', lambda t=t, op=op: nc.gpsimd.affine_select(out=t.t[:], in_=t.t[:], pattern=[[1, 128]], compare_op=op,
                                                                    fill=0.0, base=0, channel_multiplier=-1), reads=[t], writes=[t])
        S.op('dve', lambda: nc.vector.tensor_copy(self.ident_b.t[:], self.ident_f.t[:]), reads=[self.ident_f], writes=[self.ident_b])
        self.maskpair = []
        for s in range(4):
            mp = S.sb("maskpair%d" % s, [128, 2, 128], BF16)
            S.op('dve', lambda mp=mp, s=s: nc.vector.tensor_scalar(mp.t[:, 0, :], self.tri_ge.t[:], self.flg.t[:, s:s + 1], None, op0=ALU.mult),
                 reads=[self.tri_ge, self.flg], writes=[mp])
            S.op('dve', lambda mp=mp: nc.vector.tensor_copy(mp.t[:, 1, :], self.tri_le.t[:]), reads=[self.tri_le, mp], writes=[mp])
            self.maskpair.append(mp)
        self.psb = [S.ps("psb%d" % i, [128, 512], F32) for i in range(8)]
        self.psi = 0

    def bank(self, lo=0, hi=8):
        n = hi - lo
        if not hasattr(self, 'psis'):
            self.psis = {}
        i = self.psis.get((lo, hi), 0)
        self.psis[(lo, hi)] = i + 1
        return self.psb[lo + (i % n)]

    def phase_mod(self, L, cvec, w_ada, b_ada):
        S, nc = self.S, self.nc
        modrow = self.scratch("modrow%d" % L, [1, 6 * D], F32)
        mr = self.R("modrow%d" % L)
        S.push()
        ct = S.sb("ct", [128, KC], F32)
        cs = S.sb("cs", [128, KC], F32)
        S.load('sp', ct, ct.t[:], cvec.rearrange("(p c) -> p c", c=KC))
        S.op('act', lambda: nc.scalar.activation(cs.t[:], ct.t[:], AF.Silu), reads=[ct], writes=[cs])
        CW = 512
        wv = w_ada.rearrange("(p c) n -> p c n", c=KC)
        csb = S.sb("csb", [128, KC], BF16)
        S.op('dve', lambda: nc.vector.tensor_copy(csb.t[:], cs.t[:]), reads=[cs], writes=[csb])
        wb = [S.sb("wada%d" % i, [128, KC, CW], BF16) for i in range(2)]
        bb = [S.sb("bada%d" % i, [1, CW], F32) for i in range(2)]
        ob = [S.sb("oada%d" % i, [1, CW], F32) for i in range(2)]
        nchunk = 6 * D // CW
        for j in range(nchunk):
            w, b, o = wb[j % 2], bb[j % 2], ob[j % 2]
            S.load('pool', w, w.t[:], wv[:, :, j * CW:(j + 1) * CW])
            S.load('sp', b, b.t[:], b_ada[:, j * CW:(j + 1) * CW])
            pb = self.bank()
            S.mm_group([(lambda c=c, w=w, pb=pb: nc.tensor.matmul(pb.t[0:1, 0:CW], csb.t[:, c:c + 1], w.t[:, c, :], start=(c == 0), stop=(c == KC - 1)))
                        for c in range(KC)], reads=[csb, w], writes=[pb])
            S.op('dve', lambda pb=pb, b=b, o=o: nc.vector.tensor_tensor(o.t[:], pb.t[0:1, 0:CW], b.t[:], op=ALU.add), reads=[pb, b], writes=[o])
            S.store('sp', mr, modrow[:, j * CW:(j + 1) * CW], o, o.t[:])
        S.pop()
        return modrow

    def load_modT(self, L, modrow):
        S, nc = self.S, self.nc
        mt = S.sb("modT%d" % L, [128, 6, KC], F32)
        S.load('sp', mt, mt.t[:], modrow.rearrange("o (s p c) -> p (o s) c", s=6, c=KC), reads=[self.R("modrow%d" % L)])
        for s in (1, 4):
            S.op('dve', lambda s=s: nc.vector.tensor_scalar(mt.t[:, s, :], mt.t[:, s, :], 1.0, None, op0=ALU.add), reads=[mt], writes=[mt])
        return mt

    def phase_hT(self, name, x_ap, x_res, ntok, mt, s_shift, s_scale):
        S, nc = self.S, self.nc
        hT = self.scratch(name, [128, KC, ntok], BF16)
        hr = self.R(name)
        S.push()
        xb = [S.sb("xb%d" % i, [128, D], F32) for i in range(2)]
        hb = [S.sb("hb%d" % i, [128, KC, 512], BF16) for i in range(2)]
        hbA = [Res("hbA%d" % i, hb[i].t) for i in range(2)]
        nt = ntok // 128
        for t in range(nt):
            x = xb[t % 2]
            h = hb[(t // 4) % 2]
            hA = hbA[(t // 4) % 2]
            S.load('sp' if t % 2 == 0 else 'act', x, x.t[:], x_ap[t * 128:(t + 1) * 128, :], reads=[x_res] if x_res is not None else [])
            xv = x.t[:].rearrange("t (p c) -> t c p", c=KC)
            for cg in range(KC // 4):
                pb = self.bank()
                S.mm_group([(lambda j=j, pb=pb: nc.tensor.transpose(pb.t[:, j * 128:(j + 1) * 128], xv[:, cg * 4 + j, :], self.ident_f.t[:]))
                            for j in range(4)], reads=[x, self.ident_f], writes=[pb])
                for j in range(4):
                    c = cg * 4 + j
                    o = h.t[:, c, (t % 4) * 128:(t % 4 + 1) * 128]
                    i_ = pb.t[:, j * 128:(j + 1) * 128]
                    if cg % 2 == 0:
                        S.op('dve', lambda o=o, i_=i_, c=c: nc.vector.tensor_scalar(o, i_, mt.t[:, s_scale, c:c + 1], mt.t[:, s_shift, c:c + 1],
                                                                                     op0=ALU.mult, op1=ALU.add), reads=[pb, mt], writes=[h])
                    else:
                        S.op('act', lambda o=o, i_=i_, c=c: nc.scalar.activation(o, i_, AF.Identity, bias=mt.t[:, s_shift, c:c + 1],
                                                                                scale=mt.t[:, s_scale, c:c + 1]), reads=[pb, mt], writes=[hA])
            if t % 4 == 3:
                t0 = (t // 4) * 512
                S.store('sp', hr, hT[:, :, t0:t0 + 512], h, h.t[:], extra=[hA])
        S.pop()
        return hT

    def gemm(self, mode, inT, in_res, kc, tiles, w3, jobs, wcols=512, wq='pool'):
        S, nc = self.S, self.nc
        S.push()
        inb = [S.sb("inb%d" % i, [128, kc, 512], BF16) for i in range(2)]
        wb = [S.sb("wslab%d" % i, [128, kc, wcols], BF16) for i in range(2)]
        wi = 0
        for st in range(0, len(tiles), 2):
            halves = tiles[st:st + 2]
            for hi_, t0 in enumerate(halves):
                S.load('sp', inb[hi_], inb[hi_].t[:], inT[:, :, t0:t0 + 512], reads=[in_res])
            for job in jobs:
                act = [(hi_, t0) for hi_, t0 in enumerate(halves) if job['want'](t0)]
                if not act:
                    continue
                c0, ncols = job['c0'], job['ncols']
                w = wb[wi % 2]
                wi += 1
                S.load(wq, w, w.t[:, :, 0:ncols], w3[:, :, c0:c0 + ncols])
                if mode == 'B':
                    for m in range(ncols // 128):
                        for hi_, t0 in act:
                            pb = self.bank(0, 4)
                            S.mm_group([(lambda k=k, pb=pb, hi_=hi_, m=m, w=w: nc.tensor.matmul(pb.t[:, :], w.t[:, k, m * 128:(m + 1) * 128], inb[hi_].t[:, k, :],
                                                                                               start=(k == 0), stop=(k == kc - 1))) for k in range(kc)],
                                       reads=[w, inb[hi_]], writes=[pb])
                            job['epi'](c0 + m * 128, t0, pb)
                else:
                    for hi_, t0 in act:
                        for sub in range(4):
                            pb = self.bank(0, 4)
                            S.mm_group([(lambda k=k, pb=pb, hi_=hi_, sub=sub, w=w: nc.tensor.matmul(pb.t[:, 0:ncols], inb[hi_].t[:, k, sub * 128:(sub + 1) * 128], w.t[:, k, 0:ncols],
                                                                                                   start=(k == 0), stop=(k == kc - 1))) for k in range(kc)],
                                       reads=[w, inb[hi_]], writes=[pb])
                            job['epi'](c0, t0 + sub * 128, pb)
        S.pop()

    def rope_tables(self, pos_ap):
        S, nc = self.S, self.nc
        T = self.T
        self.cosT = S.sb("cosT", [32, T], F32)
        self.ssinT = S.sb("ssinT", [32, T], F32)
        S.push()
        pi_ = S.sb("pos_i", [32, T], I32)
        ang = S.sb("ang", [32, T], F32)
        tmp = S.sb("atmp", [32, T], F32)
        ki = S.sb("aki", [32, T], I32)
        kf = S.sb("akf", [32, T], F32)
        S.load('sp', pi_, pi_.t[:], pos_ap[0:1, :].to_broadcast([32, T]))
        S.op('dve', lambda: nc.vector.tensor_copy(ang.t[:], pi_.t[:]), reads=[pi_], writes=[ang])
        S.op('dve', lambda: nc.vector.tensor_scalar(ang.t[:], ang.t[:], self.cst.t[0:32, 129:130], None, op0=ALU.mult), reads=[ang, self.cst], writes=[ang])
        for which, off, dst in (("sin", 0.0, self.ssinT), ("cos", TWO_PI / 4, self.cosT)):
            S.op('dve', lambda off=off: nc.vector.tensor_scalar(tmp.t[:], ang.t[:], off, None, op0=ALU.add), reads=[ang], writes=[tmp])
            S.op('dve', lambda: nc.vector.tensor_scalar(ki.t[:], tmp.t[:], 1.0 / TWO_PI, None, op0=ALU.mult), reads=[tmp], writes=[ki])
            S.op('dve', lambda: nc.vector.tensor_copy(kf.t[:], ki.t[:]), reads=[ki], writes=[kf])
            S.op('dve', lambda: nc.vector.scalar_tensor_tensor(tmp.t[:], kf.t[:], -TWO_PI, tmp.t[:], op0=ALU.mult, op1=ALU.add), reads=[kf, tmp], writes=[tmp])
            S.op('dve', lambda: nc.vector.tensor_scalar(tmp.t[:], tmp.t[:], 3.14159, -3.14159, op0=ALU.min, op1=ALU.max), reads=[tmp], writes=[tmp])
            S.op('act', lambda dst=dst: nc.scalar.activation(dst.t[:], tmp.t[:], AF.Sin), reads=[tmp], writes=[dst])
        S.op('dve', lambda: nc.vector.tensor_scalar(self.ssinT.t[:], self.ssinT.t[:], self.cst.t[0:32, 128:129], None, op0=ALU.mult),
             reads=[self.ssinT, self.cst], writes=[self.ssinT])
        S.pop()

    def layer(self, L, x_ext, x_res, x_out, xo_res, W, cvec, seg_of, N=None, H=None):
        S, nc = self.S, self.nc
        if N is not None:
            self.N, self.H, self.T = N, H, N + H
        N, H, T = self.N, self.H, self.T
        pfx = "L%d_" % L
        modrow = self.phase_mod(L, cvec, W['w_ada'], W['b_ada'])
        mt = self.load_modT(L, modrow)
        mr = self.R("modrow%d" % L)
        hT = self.phase_hT(pfx + "hT", x_ext, x_res, T, mt, 0, 1)
        hr = self.R(pfx + "hT")
        if getattr(self, 'stop_after', None) == 'P1':
            return
        all_tiles = list(range(0, T, 512))
        main = lambda t0: t0 >= H
        w_in = W['w_in'].rearrange("(p c) n -> p c n", c=KC)

        uT = self.scratch(pfx + "uT", [128, 8, T], BF16)
        qkT = self.scratch(pfx + "qkT", [128, 48, T], BF16)
        cvT = self.scratch(pfx + "cvT", [128, 24, T], F32)
        vg = self.scratch(pfx + "vg", [T, MIXW], F32)
        Vt = self.scratch(pfx + "Vt", [T, 3 * MIXW], BF16)
        brT = self.scratch(pfx + "brT", [128, 24, T], BF16)
        mgT = self.scratch(pfx + "mgT", [128, KC, T], BF16)
        rbuf = self.scratch(pfx + "rbuf", [N, D], F32)
        x1 = self.scratch(pfx + "x1", [N, D], F32)

        S.push()
        self.rope_tables(W['pos'])
        stg_b = [S.sb("stgb%d" % i, [128, 512], BF16) for i in range(3)]
        stg_f = [S.sb("stgf%d" % i, [128, 512], F32) for i in range(3)]
        r32 = [S.sb("r32_%d" % i, [32, 512], F32) for i in range(2)]
        t32 = [S.sb("t32_%d" % i, [32, 512], F32) for i in range(2)]
        permf = self.cst
        cnt = [0]

        def epi_u(c0, t0, pb):
            o = stg_b[cnt[0] % 3]; cnt[0] += 1
            S.op('act', lambda: nc.scalar.activation(o.t[:], pb.t[:], AF.Gelu_apprx_tanh), reads=[pb], writes=[o])
            S.store('sp', self.R(pfx + "uT"), uT[:, c0 // 128, t0:t0 + 512], o, o.t[:])

        def epi_qk(c0, t0, pb):
            i = cnt[0]; cnt[0] += 1
            o = stg_b[i % 3]; r = stg_f[i % 3]; tt = t32[i % 2]; rc = r32[i % 2]
            ch = (c0 - 2048) // 128
            S.op('act', lambda: nc.scalar.copy(r.t[:], pb.t[:]), reads=[pb], writes=[r])
            p2 = self.bank(4, 8)
            S.mm_group([lambda: nc.tensor.matmul(p2.t[:, :], self.cst.t[:, 0:128], r.t[:], start=True, stop=True)], reads=[self.cst, r], writes=[p2])
            S.op('dve', lambda: nc.vector.tensor_tensor(tt.t[:], p2.t[0:32, :], self.ssinT.t[:, t0:t0 + 512], op=ALU.mult), reads=[p2, self.ssinT], writes=[tt])
            S.op('pool', lambda: nc.gpsimd.tensor_tensor(rc.t[:], r.t[0:32, :], self.cosT.t[:, t0:t0 + 512], op=ALU.mult), reads=[r, self.cosT], writes=[rc])
            S.op('pool', lambda: nc.gpsimd.tensor_copy(o.t[:, :], r.t[:, :]), reads=[r], writes=[o])
            S.op('dve', lambda: nc.vector.tensor_tensor(o.t[0:32, :], rc.t[:], tt.t[:], op=ALU.add), reads=[rc, tt, o], writes=[o])
            S.store('sp', self.R(pfx + "qkT"), qkT[:, ch, t0:t0 + 512], o, o.t[:])

        def epi_cv(c0, t0, pb):
            o = stg_f[cnt[0] % 3]; cnt[0] += 1
            S.op('act', lambda: nc.scalar.copy(o.t[:], pb.t[:]), reads=[pb], writes=[o])
            S.store('sp', self.R(pfx + "cvT"), cvT[:, (c0 - 11264) // 128, t0:t0 + 512], o, o.t[:])

        jobs = []
        for c0 in range(0, 1024, 256):
            jobs.append(dict(c0=c0, ncols=256, want=main, epi=epi_u))
        for c0 in range(2048, 2048 + 3072, 256):
            jobs.append(dict(c0=c0, ncols=256, want=main, epi=epi_qk))
        for c0 in range(2048 + 3072, 2048 + 6144, 256):
            jobs.append(dict(c0=c0, ncols=256, want=lambda t0: True, epi=epi_qk))
        for c0 in range(11264, 11264 + 3072, 256):
            jobs.append(dict(c0=c0, ncols=256, want=lambda t0: t0 >= H - 512, epi=epi_cv))
        import os
        jf = os.environ.get('KDBG_JOBS')
        if jf:
            jobs = [j for j in jobs if j['epi'].__name__ in jf.split(',')]
        self.gemm('B', hT, hr, KC, all_tiles, w_in, jobs, wcols=256)
        S.pop()
        if getattr(self, 'stop_after', None) == 'P2':
            return

        S.push()
        stg_b = [S.sb("stgb%d" % i, [128, 512], BF16) for i in range(3)]
        stg_f = [S.sb("stgf%d" % i, [128, 512], F32) for i in range(3)]

        def epi_vg(c0, t0, pb):
            o = stg_f[cnt[0] % 3]; cnt[0] += 1
            S.op('act', lambda: nc.scalar.activation(o.t[:], pb.t[:], AF.Gelu_apprx_tanh), reads=[pb], writes=[o])
            S.store('sp', self.R(pfx + "vg"), vg[t0:t0 + 128, c0 - 1024:c0 - 1024 + 512], o, o.t[:])

        def epi_V(c0, t0, pb):
            o = stg_b[cnt[0] % 3]; cnt[0] += 1
            S.op('act', lambda: nc.scalar.copy(o.t[:], pb.t[:]), reads=[pb], writes=[o])
            S.store('sp', self.R(pfx + "Vt"), Vt[t0:t0 + 128, c0 - 8192:c0 - 8192 + 512], o, o.t[:])

        jobs = []
        for c0 in range(1024, 2048, 512):
            jobs.append(dict(c0=c0, ncols=512, want=main, epi=epi_vg))
        for c0 in range(8192, 8192 + 3072, 512):
            jobs.append(dict(c0=c0, ncols=512, want=lambda t0: True, epi=epi_V))
        self.gemm('A', hT, hr, KC, all_tiles, w_in, jobs)
        S.pop()

        if getattr(self, 'stop_after', None) == 'P3':
            return
        S.push()
        lng = S.sb("lng", [128, MIXW], F32)
        lnb = S.sb("lnb", [128, MIXW], F32)
        S.load('sp', lng, lng.t[:], W['ln_v_g'][0:1, :].to_broadcast([128, MIXW]))
        S.load('sp', lnb, lnb.t[:], W['ln_v_b'][0:1, :].to_broadcast([128, MIXW]))
        wspf = S.sb("wspf", [128, 8, 128], F32)
        wspb = S.sb("wspb", [128, 8, 128], BF16)
        bsp = S.sb("bsp", [1, 8, 128], F32)
        S.load('sp', wspf, wspf.t[:], W['w_spT'].rearrange("g s t -> s g t"))
        S.load('sp', bsp, bsp.t[:], W['b_sp'].rearrange("(o g) t -> o g t", o=1))
        for g in range(8):
            S.op('dve', lambda g=g: nc.vector.tensor_tensor(wspb.t[:, g, :], wspf.t[:, g, :], self.tri_le.t[:], op=ALU.mult), reads=[wspf, self.tri_le], writes=[wspb])
        vtb = [S.sb("vt%d" % i, [128, MIXW], F32) for i in range(2)]
        vnb = [S.sb("vn%d" % i, [128, MIXW], BF16) for i in range(2)]
        vtmp = S.sb("vtmp", [128, MIXW], F32)
        utb = [S.sb("ut%d" % i, [128, 8, 128], BF16) for i in range(2)]
        brb = [S.sb("bra%d" % i, [128, 8, 128], BF16) for i in range(2)]
        stats = S.sb("bnst", [128, 2, 6], F32)
        mv = S.sb("bnmv", [128, 2], F32)
        rstd = S.sb("rstd", [128, 1], F32)
        for ci in range(N // 128):
            e0 = H + ci * 128
            vt, vn, ut, br = vtb[ci % 2], vnb[ci % 2], utb[ci % 2], brb[ci % 2]
            S.load('sp', vt, vt.t[:], vg[e0:e0 + 128, :], reads=[self.R(pfx + "vg")])
            S.load('act', ut, ut.t[:], uT[:, :, e0:e0 + 128], reads=[self.R(pfx + "uT")])
            for hh in range(2):
                S.op('dve', lambda hh=hh: nc.vector.bn_stats(stats.t[:, hh, :], vt.t[:, hh * 512:(hh + 1) * 512]), reads=[vt], writes=[stats])
            S.op('dve', lambda: nc.vector.bn_aggr(mv.t[:], stats.t[:].rearrange("p a b -> p (a b)")), reads=[stats], writes=[mv])
            S.op('act', lambda: nc.scalar.activation(rstd.t[:], mv.t[:, 1:2], AF.Sqrt, bias=self.epsc.t[:], scale=1.0), reads=[mv, self.epsc], writes=[rstd])
            S.op('dve', lambda: nc.vector.reciprocal(rstd.t[:], rstd.t[:]), reads=[rstd], writes=[rstd])
            S.op('dve', lambda: nc.vector.tensor_scalar(vtmp.t[:], vt.t[:], mv.t[:, 0:1], rstd.t[:], op0=ALU.subtract, op1=ALU.mult), reads=[vt, mv, rstd], writes=[vtmp])
            S.op('pool', lambda: nc.gpsimd.tensor_tensor(vtmp.t[:], vtmp.t[:], lng.t[:], op=ALU.mult), reads=[vtmp, lng], writes=[vtmp])
            S.op('dve', lambda: nc.vector.tensor_tensor(vn.t[:], vtmp.t[:], lnb.t[:], op=ALU.add), reads=[vtmp, lnb], writes=[vn])
            for half in range(2):
                pb = self.bank()
                for gg in range(4):
                    g = half * 4 + gg
                    S.mm_group([lambda g=g, gg=gg, pb=pb: nc.tensor.matmul(pb.t[:, gg * 128:(gg + 1) * 128], vn.t[:, g * 128:(g + 1) * 128], wspb.t[:, g, :], start=True, stop=False),
                                lambda g=g, gg=gg, pb=pb: nc.tensor.matmul(pb.t[:, gg * 128:(gg + 1) * 128], self.ones_f.t[0:1, :], bsp.t[0:1, g, :], start=False, stop=True)],
                               reads=[vn, wspb, self.ones_f, bsp], writes=[pb])
                S.op('dve', lambda half=half, pb=pb: nc.vector.tensor_tensor(br.t[:, half * 4:(half + 1) * 4, :], pb.t[:, :].rearrange("p (g t) -> p g t", g=4),
                                                                            ut.t[:, half * 4:(half + 1) * 4, :], op=ALU.mult), reads=[pb, ut], writes=[br])
            S.store('sp', self.R(pfx + "brT"), brT[:, 0:8, e0:e0 + 128], br, br.t[:])
        S.pop()

        if getattr(self, 'stop_after', None) == 'P4':
            return
        S.push()
        kTb = [S.sb("kTb%d" % i, [128, T], BF16) for i in range(2)]
        qTb = [S.sb("qTb%d" % i, [128, N], BF16) for i in range(2)]
        Vhb = [S.sb("Vhb%d" % i, [128, T // 128, 128], BF16) for i in range(2)]
        acc = S.sb("accOL", [128, 2, N], F32)
        Etb = [S.sb("Et%d" % i, [128, 2, 128], BF16) for i in range(4)]
        Emb = [S.sb("Em%d" % i, [128, 2, 128], BF16) for i in range(4)]
        rec = S.sb("recL", [128, N], F32)
        ao = S.sb("attn_o", [128, N], BF16)
        it = 0
        for h in range(8):
            for g, dil in enumerate((1, 4, 16)):
                hh = g * 8 + h
                kT, qT, Vh = kTb[(h * 3 + g) % 2], qTb[(h * 3 + g) % 2], Vhb[(h * 3 + g) % 2]
                S.load('sp', kT, kT.t[:], qkT[:, 24 + hh, :], reads=[self.R(pfx + "qkT")])
                S.load('act', qT, qT.t[:], qkT[:, hh, H:H + N], reads=[self.R(pfx + "qkT")])
                nT = T // (128 * dil)
                vsrc = Vt[:, hh * 128:(hh + 1) * 128].rearrange("(T k r) d -> r k T d", k=128, r=dil)
                for r in range(dil):
                    S.load('sp' if r % 2 == 0 else 'act', Vh, Vh.t[:, r * nT:(r + 1) * nT, :], vsrc[r], reads=[self.R(pfx + "Vt")])
                for r in range(dil):
                    for Tq in range(H // (128 * dil), nT):
                        Et, Em = Etb[it % 4], Emb[it % 4]
                        it += 1
                        q0 = Tq * 128 * dil + r - H
                        qs = qT.t[:, q0:q0 + 127 * dil + 1:dil]
                        pb = self.bank(0, 4)
                        fns = []
                        for j, Tk in enumerate((Tq - 1, Tq)):
                            k0 = Tk * 128 * dil + r
                            ks = kT.t[:, k0:k0 + 127 * dil + 1:dil]
                            fns.append(lambda j=j, ks=ks, pb=pb: nc.tensor.matmul(pb.t[:, j * 128:(j + 1) * 128], ks, qs, start=True, stop=True))
                        S.mm_group(fns, reads=[kT, qT], writes=[pb])
                        S.op('act', lambda pb=pb, Et=Et: nc.scalar.activation(Et.t[:].rearrange("p a b -> p (a b)"), pb.t[:, 0:256], AF.Exp, scale=128.0 ** -0.5),
                             reads=[pb], writes=[Et])
                        seg = seg_of((Tq - 1) * 128 * dil + r)
                        mp = self.maskpair[seg]
                        S.op('pool', lambda Et=Et, Em=Em, mp=mp: nc.gpsimd.tensor_tensor(Em.t[:], Et.t[:], mp.t[:], op=ALU.mult), reads=[Et, mp], writes=[Em])
                        p2 = self.bank(4, 8)
                        fns = []
                        for j, Tk in enumerate((Tq - 1, Tq)):
                            fns.append(lambda j=j, Tk=Tk, p2=p2, Em=Em: nc.tensor.matmul(p2.t[:, 0:128], Vh.t[:, r * nT + Tk, :], Em.t[:, j, :], start=(j == 0), stop=(j == 1)))
                        S.mm_group(fns, reads=[Vh, Em], writes=[p2])
                        fns = []
                        for j in range(2):
                            fns.append(lambda j=j, p2=p2, Em=Em: nc.tensor.matmul(p2.t[:, 128:256], self.ones_b.t[:], Em.t[:, j, :], start=(j == 0), stop=(j == 1)))
                        S.mm_group(fns, reads=[self.ones_b, Em], writes=[p2])
                        dst = acc.t[:, :, q0:q0 + 127 * dil + 1:dil]
                        src = p2.t[:, 0:256].rearrange("p (a b) -> p a b", a=2)
                        if g == 0:
                            S.op('act', lambda dst=dst, src=src: nc.scalar.copy(dst, src), reads=[p2], writes=[acc])
                        else:
                            S.op('dve', lambda dst=dst, src=src: nc.vector.tensor_tensor(dst, dst, src, op=ALU.add), reads=[p2, acc], writes=[acc])
            S.op('dve', lambda: nc.vector.reciprocal(rec.t[:], acc.t[:, 1, :]), reads=[acc], writes=[rec])
            S.op('dve', lambda: nc.vector.tensor_tensor(ao.t[:], acc.t[:, 0, :], rec.t[:], op=ALU.mult), reads=[acc, rec], writes=[ao])
            S.store('sp', self.R(pfx + "brT"), brT[:, 8 + h, H:H + N], ao, ao.t[:])
        S.pop()

        if getattr(self, 'stop_after', None) == 'P5':
            return
        S.push()
        cw = S.sb("convw", [128, 8, 3], F32)
        S.load('sp', cw, cw.t[:], W['conv_w'][:, :, :])
        gcb = [S.sb("gc%d" % i, [128, 514], F32) for i in range(2)]
        xib = [S.sb("xi%d" % i, [128, 514], F32) for i in range(2)]
        gbb = [S.sb("gb%d" % i, [128, 512], F32) for i in range(2)]
        yb = [S.sb("cy%d" % i, [128, 512], F32) for i in range(2)]
        ob = [S.sb("co%d" % i, [128, 512], BF16) for i in range(2)]
        it = 0
        cr = self.R(pfx + "cvT")
        for j in range(8):
            for t0 in range(H, T, 512):
                gc, xi, gb, y, o = gcb[it % 2], xib[it % 2], gbb[it % 2], yb[it % 2], ob[it % 2]
                it += 1
                S.load('sp', gc, gc.t[:], cvT[:, 8 + j, t0 - 2:t0 + 512], reads=[cr])
                S.load('act', xi, xi.t[:], cvT[:, 16 + j, t0 - 2:t0 + 512], reads=[cr])
                S.load('sp', gb, gb.t[:], cvT[:, j, t0:t0 + 512], reads=[cr])
                S.op('pool', lambda: nc.gpsimd.tensor_tensor(gc.t[:], gc.t[:], xi.t[:], op=ALU.mult), reads=[gc, xi], writes=[gc])
                sg = seg_of(t0 - 1)
                if sg != seg_of(t0):
                    S.op('pool', lambda sg=sg: nc.gpsimd.tensor_scalar(gc.t[:, 0:2], gc.t[:, 0:2], self.flg.t[:, sg:sg + 1], None, op0=ALU.mult), reads=[gc, self.flg], writes=[gc])
                S.op('dve', lambda: nc.vector.tensor_scalar(y.t[:], gc.t[:, 0:512], cw.t[:, j, 0:1], None, op0=ALU.mult), reads=[gc, cw], writes=[y])
                S.op('dve', lambda: nc.vector.scalar_tensor_tensor(y.t[:], gc.t[:, 1:513], cw.t[:, j, 1:2], y.t[:], op0=ALU.mult, op1=ALU.add), reads=[gc, cw, y], writes=[y])
                S.op('dve', lambda: nc.vector.scalar_tensor_tensor(y.t[:], gc.t[:, 2:514], cw.t[:, j, 2:3], y.t[:], op0=ALU.mult, op1=ALU.add), reads=[gc, cw, y], writes=[y])
                S.op('pool', lambda: nc.gpsimd.tensor_tensor(o.t[:], y.t[:], gb.t[:], op=ALU.mult), reads=[y, gb], writes=[o])
                S.store('sp', self.R(pfx + "brT"), brT[:, 16 + j, t0:t0 + 512], o, o.t[:])
        S.pop()

        if getattr(self, 'stop_after', None) == 'P6':
            return
        S.push()
        hb = [S.sb("mhT%d" % i, [128, KC, 512], BF16) for i in range(1)]
        bb = [S.sb("mbr%d" % i, [128, 24, 512], BF16) for i in range(1)]
        wgb = [S.sb("wg%d" % i, [128, KC, 512], BF16) for i in range(2)]
        wbb = [S.sb("wbr%d" % i, [128, 8, 512], BF16) for i in range(2)]
        bg = S.sb("bgate", [128, 3, KC], F32)
        S.load('sp', bg, bg.t[:], W['b_gate'][:, :, :])
        gtb = [S.sb("gate%d" % i, [128, 512], F32) for i in range(2)]
        mac = [S.sb("macc%d" % i, [128, 4, 512], F32) for i in range(2)]
        mob = [S.sb("mout%d" % i, [128, 4, 512], BF16) for i in range(2)]
        wgv = W['w_gate'].rearrange("g (p c) n -> g p c n", c=KC)
        wbv = W['w_branch'].rearrange("g (c p) n -> g p c n", p=128)
        wi = 0
        gi = 0
        for ti, t0 in enumerate(range(H, T, 512)):
            h_, b_ = hb[0], bb[0]
            S.load('sp', h_, h_.t[:], hT[:, :, t0:t0 + 512], reads=[hr])
            S.load('act', b_, b_.t[:], brT[:, :, t0:t0 + 512], reads=[self.R(pfx + "brT")])
            for ms in range(8):
                ma, mo = mac[ms % 2], mob[ms % 2]
                for g in range(3):
                    wg, wbr = wgb[wi % 2], wbb[wi % 2]
                    wi += 1
                    S.load('pool', wg, wg.t[:], wgv[g, :, :, ms * 512:(ms + 1) * 512])
                    S.load('pool', wbr, wbr.t[:], wbv[g, :, :, ms * 512:(ms + 1) * 512])
                    for m in range(4):
                        gt = gtb[gi % 2]
                        gi += 1
                        pb = self.bank(0, 4)
                        S.mm_group([(lambda k=k, pb=pb, wg=wg, m=m: nc.tensor.matmul(pb.t[:, :], wg.t[:, k, m * 128:(m + 1) * 128], h_.t[:, k, :], start=(k == 0), stop=(k == KC - 1)))
                                    for k in range(KC)], reads=[wg, h_], writes=[pb])
                        cidx = ms * 4 + m
                        S.op('act', lambda pb=pb, gt=gt, g=g, cidx=cidx: nc.scalar.activation(gt.t[:], pb.t[:], AF.Sigmoid, bias=bg.t[:, g, cidx:cidx + 1], scale=1.0),
                             reads=[pb, bg], writes=[gt])
                        p2 = self.bank(4, 8)
                        S.mm_group([(lambda k=k, p2=p2, wbr=wbr, m=m, g=g: nc.tensor.matmul(p2.t[:, :], wbr.t[:, k, m * 128:(m + 1) * 128], b_.t[:, g * 8 + k, :], start=(k == 0), stop=(k == 7)))
                                    for k in range(8)], reads=[wbr, b_], writes=[p2])
                        if g == 0:
                            S.op('dve', lambda p2=p2, gt=gt, ma=ma, m=m: nc.vector.tensor_tensor(ma.t[:, m, :], p2.t[:], gt.t[:], op=ALU.mult), reads=[p2, gt], writes=[ma])
                        else:
                            S.op('dve', lambda p2=p2, gt=gt, m=m: nc.vector.tensor_tensor(gt.t[:], p2.t[:], gt.t[:], op=ALU.mult), reads=[p2, gt], writes=[gt])
                            if g == 1:
                                S.op('pool', lambda gt=gt, ma=ma, m=m: nc.gpsimd.tensor_tensor(ma.t[:, m, :], ma.t[:, m, :], gt.t[:], op=ALU.add), reads=[gt, ma], writes=[ma])
                            else:
                                S.op('pool', lambda gt=gt, ma=ma, mo=mo, m=m: nc.gpsimd.tensor_tensor(mo.t[:, m, :], ma.t[:, m, :], gt.t[:], op=ALU.add), reads=[gt, ma], writes=[mo])
                S.store('sp', self.R(pfx + "mgT"), mgT[:, ms * 4:(ms + 1) * 4, t0:t0 + 512], mo, mo.t[:])
        S.pop()

        if getattr(self, 'stop_after', None) == 'P7':
            return
        def resid_gemm(inT, in_res, kcs, w3_of, tiles, toff, xsrc, xsrc_res, xoff, gslot, outs):
            S.push()
            gbc = S.sb("gbc", [128, D], F32)
            S.load('sp', gbc, gbc.t[:], modrow[0:1, gslot * D:(gslot + 1) * D].to_broadcast([128, D]), reads=[mr])
            xtb = [S.sb("xres%d" % i, [128, 512], F32) for i in range(3)]
            rtb = [S.sb("rres%d" % i, [128, 512], F32) for i in range(3)]
            c2 = [0]
            for part, (out_ap, out_name) in enumerate(outs):
                def epi(c0, t0, pb, part=part, out_ap=out_ap, out_name=out_name):
                    i = c2[0]; c2[0] += 1
                    xt, rt = xtb[i % 3], rtb[i % 3]
                    m0 = t0 - toff
                    if part == 0:
                        S.load('act', xt, xt.t[:], xsrc[xoff + m0:xoff + m0 + 128, c0:c0 + 512], reads=[xsrc_res] if xsrc_res is not None else [])
                    S.op('dve', lambda: nc.vector.tensor_tensor(rt.t[:], pb.t[:], gbc.t[:, c0:c0 + 512], op=ALU.mult), reads=[pb, gbc], writes=[rt])
                    if part == 0:
                        S.op('dve', lambda: nc.vector.scalar_tensor_tensor(rt.t[:], xt.t[:], ALPHA, rt.t[:], op0=ALU.mult, op1=ALU.add), reads=[xt, rt], writes=[rt])
                    S.store('sp', self.R(out_name), out_ap[m0:m0 + 128, c0:c0 + 512], rt, rt.t[:])
                jobs = [dict(c0=c0, ncols=512, want=lambda t0: True, epi=epi) for c0 in range(0, D, 512)]
                self.gemm('A', inT(part), in_res, kcs, tiles, w3_of(part), jobs)
            S.pop()

        resid_gemm(lambda part: mgT, self.R(pfx + "mgT"), KC, lambda part: W['w_o'].rearrange("(c p) n -> p c n", p=128),
                   [t for t in all_tiles if main(t)], H, x_ext, x_res, H, 2, [(rbuf, pfx + "rbuf")])

        def ln_phase(parts, g_ap, b_ap, out_ap, out_res):
            S.push()
            lg = S.sb("lng", [128, D], F32)
            lb = S.sb("lnb", [128, D], F32)
            S.load('sp', lg, lg.t[:], g_ap[0:1, :].to_broadcast([128, D]))
            S.load('sp', lb, lb.t[:], b_ap[0:1, :].to_broadcast([128, D]))
            rtb = [[S.sb("lnr%d_%d" % (p, i), [128, D], F32) for i in range(2 if len(parts) == 1 else 1)] for p in range(len(parts))]
            otb = [S.sb("lno%d" % i, [128, D], F32) for i in range(2)]
            st = S.sb("lnst", [128, 8, 6], F32)
            mv_ = S.sb("lnmv", [128, 2], F32)
            rs = S.sb("lnrs", [128, 1], F32)
            for ti in range(N // 128):
                rts = [rtb[p][ti % len(rtb[p])] for p in range(len(parts))]
                for p, (pap, pname) in enumerate(parts):
                    S.load('sp' if p % 2 == 0 else 'act', rts[p], rts[p].t[:], pap[ti * 128:(ti + 1) * 128, :], reads=[self.R(pname)])
                r0 = rts[0]
                for p in range(1, len(parts)):
                    S.op('pool' if p % 2 else 'dve', (lambda p=p: nc.gpsimd.tensor_tensor(r0.t[:], r0.t[:], rts[p].t[:], op=ALU.add)) if p % 2 else
                         (lambda p=p: nc.vector.tensor_tensor(r0.t[:], r0.t[:], rts[p].t[:], op=ALU.add)), reads=[r0, rts[p]], writes=[r0])
                o = otb[ti % 2]
                for c in range(8):
                    S.op('dve', lambda c=c: nc.vector.bn_stats(st.t[:, c, :], r0.t[:, c * 512:(c + 1) * 512]), reads=[r0], writes=[st])
                S.op('dve', lambda: nc.vector.bn_aggr(mv_.t[:], st.t[:].rearrange("p a b -> p (a b)")), reads=[st], writes=[mv_])
                S.op('act', lambda: nc.scalar.activation(rs.t[:], mv_.t[:, 1:2], AF.Sqrt, bias=self.epsc.t[:], scale=1.0), reads=[mv_, self.epsc], writes=[rs])
                S.op('dve', lambda: nc.vector.reciprocal(rs.t[:], rs.t[:]), reads=[rs], writes=[rs])
                S.op('dve', lambda: nc.vector.tensor_scalar(r0.t[:], r0.t[:], mv_.t[:, 0:1], rs.t[:], op0=ALU.subtract, op1=ALU.mult), reads=[r0, mv_, rs], writes=[r0])
                S.op('pool', lambda: nc.gpsimd.tensor_tensor(r0.t[:], r0.t[:], lg.t[:], op=ALU.mult), reads=[r0, lg], writes=[r0])
                S.op('dve', lambda: nc.vector.tensor_tensor(o.t[:], r0.t[:], lb.t[:], op=ALU.add), reads=[r0, lb], writes=[o])
                S.store('sp', out_res, out_ap[ti * 128:(ti + 1) * 128, :], o, o.t[:])
            S.pop()

        ln_phase([(rbuf, pfx + "rbuf")], W['ln1_g'], W['ln1_b'], x1, self.R(pfx + "x1"))

        if getattr(self, 'stop_after', None) == 'P9':
            return
        h2T = self.phase_hT(pfx + "h2T", x1, self.R(pfx + "x1"), N, mt, 3, 4)
        h2r = self.R(pfx + "h2T")
        ptiles = list(range(0, N, 512))
        qpT = self.scratch(pfx + "qpT", [128, 8, N], BF16)
        GT = self.scratch(pfx + "GT", [128, 128, N], BF16)
        cfT = self.scratch(pfx + "cfT", [128, 128, N], BF16)
        parts = [self.scratch(pfx + "yp%d" % i, [N, D], F32) for i in range(4)]

        S.push()
        stg_b = [S.sb("stgb%d" % i, [128, 512], BF16) for i in range(3)]

        def epi_q(c0, t0, pb):
            o = stg_b[cnt[0] % 3]; cnt[0] += 1
            S.op('act', lambda: nc.scalar.copy(o.t[:], pb.t[:]), reads=[pb], writes=[o])
            S.store('sp', self.R(pfx + "qpT"), qpT[:, c0 // 128, t0:t0 + 512], o, o.t[:])
        self.gemm('B', h2T, h2r, KC, ptiles, W['w_pq'].rearrange("(p c) n -> p c n", c=KC),
                  [dict(c0=c0, ncols=512, want=lambda t0: True, epi=epi_q) for c0 in (0, 512)])
        S.pop()

        S.push()
        kbf = S.sb("keysf", [128, 8, 256], F32)
        kbd = S.sb("keysbd", [128, 8, 256], BF16)
        S.op('pool', lambda: nc.gpsimd.memset(kbf.t[:], 0.0), writes=[kbf])
        S.load('sp', kbf, kbf.t[0:64, :, 0:128], W['keysT'][:, 0, :, :].rearrange("h d n -> d h n"), reads=[kbf])
        S.load('sp', kbf, kbf.t[64:128, :, 128:256], W['keysT'][:, 1, :, :].rearrange("h d n -> d h n"), reads=[kbf])
        S.op('dve', lambda: nc.vector.tensor_copy(kbd.t[:], kbf.t[:]), reads=[kbf], writes=[kbd])
        qtb = [S.sb("qpt%d" % i, [128, 8, 128], BF16) for i in range(2)]
        sc = S.sb("scores", [128, 8, 256], F32)
        v16 = S.sb("v16", [128, 16, 16], F32)
        scr = S.sb("mscr", [128, 256], F32)
        cand = S.sb("cand", [128, 256], F32)
        c16 = S.sb("c16", [128, 16], F32)
        e16 = S.sb("e16", [128, 16], F32)
        negm = S.sb("negm", [128, 16], F32)
        zz = S.sb("zz", [128, 8], F32)
        rz = S.sb("rz", [128, 8], F32)
        tau = S.sb("tau", [128, 8], F32)
        nmx = S.sb("nmx", [128, 8], F32)
        a1 = S.sb("a1", [128, 8, 128], F32)
        a2 = S.sb("a2", [128, 8, 128], F32)
        th = S.sb("theta", [128, 8, 128], F32)
        tmb = [S.sb("tmpg%d" % i, [128, 4, 128], F32) for i in range(4)]
        t2b = [S.sb("tmpb%d" % i, [128, 4, 128], BF16) for i in range(4)]
        gtt = [S.sb("GTt%d" % i, [128, 128, 128], BF16) for i in range(2)]
        gi = 0
        for ti in range(N // 128):
            qt = qtb[ti % 2]
            S.load('sp', qt, qt.t[:], qpT[:, :, ti * 128:(ti + 1) * 128], reads=[self.R(pfx + "qpT")])
            for hp in range(4):
                pb = self.bank()
                S.mm_group([(lambda j=j, pb=pb: nc.tensor.matmul(pb.t[:, j * 256:(j + 1) * 256], qt.t[:, hp * 2 + j, :], kbd.t[:, hp * 2 + j, :], start=True, stop=True))
                            for j in range(2)], reads=[qt, kbd], writes=[pb])
                S.op('act', lambda pb=pb, hp=hp: nc.scalar.copy(sc.t[:, hp * 2:hp * 2 + 2, :].rearrange("p a b -> p (a b)"), pb.t[:, :]), reads=[pb], writes=[sc])
            for h in range(8):
                for p in range(2):
                    sv = sc.t[:, h, p * 128:(p + 1) * 128]
                    S.op('dve', lambda sv=sv, h=h, p=p: nc.vector.max(v16.t[:, h * 2 + p, 0:8], sv), reads=[sc], writes=[v16])
                    S.op('dve', lambda sv=sv, h=h, p=p: nc.vector.match_replace(scr.t[:, 0:128], v16.t[:, h * 2 + p, 0:8], sv, NEG), reads=[sc, v16], writes=[scr])
                    S.op('dve', lambda h=h, p=p: nc.vector.max(v16.t[:, h * 2 + p, 8:16], scr.t[:, 0:128]), reads=[scr], writes=[v16])
                S.op('dve', lambda h=h: nc.vector.tensor_tensor(cand.t[:].rearrange("p (a b) -> p a b", a=16), v16.t[:, h * 2, :].unsqueeze(2).to_broadcast([128, 16, 16]),
                                                              v16.t[:, h * 2 + 1, :].unsqueeze(1).to_broadcast([128, 16, 16]), op=ALU.add), reads=[v16], writes=[cand])
                S.op('dve', lambda: nc.vector.max(c16.t[:, 0:8], cand.t[:]), reads=[cand], writes=[c16])
                S.op('dve', lambda: nc.vector.match_replace(scr.t[:], c16.t[:, 0:8], cand.t[:], NEG), reads=[cand, c16], writes=[scr])
                S.op('dve', lambda: nc.vector.max(c16.t[:, 8:16], scr.t[:]), reads=[scr], writes=[c16])
                S.op('dve', lambda h=h: nc.vector.tensor_scalar(tau.t[:, h:h + 1], c16.t[:, 15:16], -2.0e-5, None, op0=ALU.add), reads=[c16], writes=[tau])
                S.op('dve', lambda h=h: nc.vector.tensor_scalar(nmx.t[:, h:h + 1], c16.t[:, 0:1], -1.0, None, op0=ALU.mult), reads=[c16], writes=[nmx])
                S.op('act', lambda h=h: nc.scalar.activation(e16.t[:], c16.t[:], AF.Exp, bias=nmx.t[:, h:h + 1], scale=1.0, accum_out=zz.t[:, h:h + 1]),
                     reads=[c16, nmx], writes=[e16, zz])
            S.op('dve', lambda: nc.vector.reciprocal(rz.t[:], zz.t[:]), reads=[zz], writes=[rz])
            S.op('dve', lambda: nc.vector.tensor_scalar(negm.t[:], v16.t[:, :, 0], -1.0, None, op0=ALU.mult), reads=[v16], writes=[negm])
            for h in range(8):
                S.op('act', lambda h=h: nc.scalar.activation(a1.t[:, h, :], sc.t[:, h, 0:128], AF.Exp, bias=negm.t[:, 2 * h:2 * h + 1], scale=1.0), reads=[sc, negm], writes=[a1])
                S.op('act', lambda h=h: nc.scalar.activation(a2.t[:, h, :], sc.t[:, h, 128:256], AF.Exp, bias=negm.t[:, 2 * h + 1:2 * h + 2], scale=1.0), reads=[sc, negm], writes=[a2])
                S.op('dve', lambda h=h: nc.vector.tensor_scalar(a1.t[:, h, :], a1.t[:, h, :], rz.t[:, h:h + 1], None, op0=ALU.mult), reads=[a1, rz], writes=[a1])
                S.op('dve', lambda h=h: nc.vector.tensor_scalar(th.t[:, h, :], sc.t[:, h, 0:128], -1.0, tau.t[:, h:h + 1], op0=ALU.mult, op1=ALU.add), reads=[sc, tau], writes=[th])
            gt_ = gtt[ti % 2]
            for ig in range(32):
                pb = self.bank()
                fns = []
                used = []
                for h in range(8):
                    tm, t2 = tmb[gi % 4], t2b[gi % 4]
                    gi += 1
                    for ii in range(4):
                        i = ig * 4 + ii
                        S.op('dve', lambda h=h, i=i, ii=ii, tm=tm: nc.vector.scalar_tensor_tensor(tm.t[:, ii, :], sc.t[:, h, 128:256], th.t[:, h, i:i + 1], a2.t[:, h, :],
                                                                                                 op0=ALU.is_ge, op1=ALU.mult), reads=[sc, th, a2], writes=[tm])
                    S.op('pool', lambda h=h, tm=tm, t2=t2: nc.gpsimd.tensor_tensor(t2.t[:], tm.t[:], a1.t[:, h, ig * 4:ig * 4 + 4].unsqueeze(2).to_broadcast([128, 4, 128]), op=ALU.mult),
                         reads=[tm, a1], writes=[t2])
                    S.mm_group([(lambda ii=ii, t2=t2, pb=pb, h=h: nc.tensor.matmul(pb.t[:, ii * 128:(ii + 1) * 128], t2.t[:, ii, :], self.ident_b.t[:], start=(h == 0 and ii == 0), stop=(h == 7 and ii == 3),
                                                                                  skip_group_check=True)) for ii in range(4)], reads=[t2, self.ident_b], writes=[pb])
                S.op('act', lambda pb=pb, ig=ig: nc.scalar.copy(gt_.t[:, ig * 4:(ig + 1) * 4, :].rearrange("p a b -> p (a b)"), pb.t[:, :]), reads=[pb], writes=[gt_])
            S.store('sp', self.R(pfx + "GT"), GT[:, :, ti * 128:(ti + 1) * 128], gt_, gt_.t[:])
        S.pop()

        if getattr(self, 'stop_after', None) == 'Q2':
            return
        S.push()
        stg_f = [S.sb("stgf%d" % i, [128, 512], F32) for i in range(3)]
        stg_b = [S.sb("stgb%d" % i, [128, 512], BF16) for i in range(3)]
        gld = [S.sb("gld%d" % i, [128, 512], BF16) for i in range(3)]

        def epi_e(c0, t0, pb):
            i = cnt[0]; cnt[0] += 1
            a, o, gl = stg_f[i % 3], stg_b[i % 3], gld[i % 3]
            S.load('act', gl, gl.t[:], GT[:, c0 // 128, t0:t0 + 512], reads=[self.R(pfx + "GT")])
            S.op('act', lambda: nc.scalar.activation(a.t[:], pb.t[:], AF.Gelu_apprx_tanh), reads=[pb], writes=[a])
            S.op('dve', lambda: nc.vector.tensor_tensor(o.t[:], a.t[:], gl.t[:], op=ALU.mult), reads=[a, gl], writes=[o])
            S.store('sp', self.R(pfx + "cfT"), cfT[:, c0 // 128, t0:t0 + 512], o, o.t[:])
        self.gemm('B', h2T, h2r, KC, ptiles, W['w_uT'].rearrange("(p c) n -> p c n", c=KC),
                  [dict(c0=c0, ncols=512, want=lambda t0: True, epi=epi_e) for c0 in range(0, 16384, 512)])
        S.pop()

        if getattr(self, 'stop_after', None) == 'Q3':
            return
        wvv = W['w_v'].rearrange("(c p) n -> p c n", p=128)
        resid_gemm(lambda part: cfT[:, part * 32:(part + 1) * 32, :], self.R(pfx + "cfT"), 32, lambda part: wvv[:, part * 32:(part + 1) * 32, :],
                   ptiles, 0, x1, self.R(pfx + "x1"), 0, 5, [(parts[i], pfx + "yp%d" % i) for i in range(4)])
        ln_phase([(parts[i], pfx + "yp%d" % i) for i in range(4)], W['ln2_g'], W['ln2_b'], x_out, xo_res)


def _consts():
    c = np.zeros((128, 256), np.float32)
    for m in range(32):
        c[(m + 16) % 32, m] = 1.0
    c[0:16, 128] = -1.0
    c[16:32, 128] = 1.0
    half = 16
    inv = (np.float32(500000.0) ** (-np.arange(half, dtype=np.float32) / np.float32(half))).astype(np.float32)
    c[0:16, 129] = inv
    c[16:32, 129] = inv
    return c


WNAMES = ['w_ada', 'b_ada', 'w_in', 'w_gate', 'b_gate', 'ln_v_g', 'ln_v_b', 'w_spT', 'b_sp', 'conv_w', 'w_branch', 'w_o',
          'ln1_g', 'ln1_b', 'w_pq', 'keysT', 'w_uT', 'w_v', 'ln2_g', 'ln2_b']
WSHAPES = {'w_ada': [D, 6 * D], 'b_ada': [1, 6 * D], 'w_in': [D, 14336], 'w_gate': [3, D, D], 'b_gate': [128, 3, KC],
           'ln_v_g': [1, MIXW], 'ln_v_b': [1, MIXW], 'w_spT': [8, 128, 128], 'b_sp': [8, 128], 'conv_w': [128, 8, 3],
           'w_branch': [3, MIXW, D], 'w_o': [D, D], 'ln1_g': [1, D], 'ln1_b': [1, D], 'w_pq': [D, 1024],
           'keysT': [8, 2, 64, 128], 'w_uT': [D, 16384], 'w_v': [16384, D], 'ln2_g': [1, D], 'ln2_b': [1, D]}


def build_single_layer(debug=(), stop_after=None):
    N, H = CH, CH
    B = Builder(N, H, 1, debug=debug)
    B.stop_after = stop_after
    x_ext = B.din("x_ext", [N + H, D])
    cvec = B.din("cvec", [D])
    pos = B.din("pos", [1, N + H], I32)
    consts = B.din("consts", [128, 256])
    flags = B.din("flags", [128, 4])
    class LazyW(dict):
        def __missing__(self, n):
            self[n] = B.din(n, WSHAPES[n])
            return self[n]
    W = LazyW()
    W['pos'] = pos
    if stop_after is None:
        for n in WNAMES:
            W[n]
    x_out = B.dout("x_out", [N, D])
    B.setup_consts(consts, flags)
    B.layer(0, x_ext, None, x_out, B.R("x_out"), W, cvec, lambda e: e // CH)
    B.S.finish()
    return B


def host_layer_weights(inp, l):
    f = lambda a: np.ascontiguousarray(np.asarray(a, dtype=np.float32))
    return {
        'w_ada': f(inp['w_ada'][l]), 'b_ada': f(inp['b_ada'][l]).reshape(1, -1), 'w_in': f(inp['w_in'][l]),
        'w_gate': f(inp['w_gate'][l]), 'b_gate': f(np.transpose(np.asarray(inp['b_gate'][l]).reshape(3, KC, 128), (2, 0, 1))),
        'ln_v_g': f(inp['ln_v_g'][l]).reshape(1, -1), 'ln_v_b': f(inp['ln_v_b'][l]).reshape(1, -1),
        'w_spT': f(np.transpose(np.asarray(inp['w_sp'][l]), (0, 2, 1))), 'b_sp': f(inp['b_sp'][l]),
        'conv_w': f(np.transpose(np.asarray(inp['conv_w'][l]).reshape(3, 8, 128), (2, 1, 0))), 'w_branch': f(inp['w_branch'][l]), 'w_o': f(inp['w_o'][l]),
        'ln1_g': f(inp['ln1_g'][l]).reshape(1, -1), 'ln1_b': f(inp['ln1_b'][l]).reshape(1, -1),
        'w_pq': f(inp['w_pq'][l]), 'keysT': f(np.transpose(np.asarray(inp['sub_keys'][l]), (0, 1, 3, 2))),
        'w_uT': f(np.asarray(inp['w_u'][l]).T), 'w_v': f(inp['w_v'][l]),
        'ln2_g': f(inp['ln2_g'][l]).reshape(1, -1), 'ln2_b': f(inp['ln2_b'][l]).reshape(1, -1),
    }


def core_inputs(xfull, cfull, posfull, core):
    b, j = core // 4, core % 4
    main = xfull[b, j * CH:(j + 1) * CH]
    if j > 0:
        halo = xfull[b, (j - 1) * CH:j * CH]
        ph = posfull[b, (j - 1) * CH:j * CH]
    else:
        halo = np.zeros_like(main)
        ph = np.zeros(CH, np.int32)
    flags = np.zeros((128, 4), np.float32)
    flags[:, 0] = 1.0 if j > 0 else 0.0
    flags[:, 1:] = 1.0
    return {"x_ext": np.ascontiguousarray(np.concatenate([halo, main], 0)),
            "cvec": np.ascontiguousarray(cfull[b]),
            "pos": np.ascontiguousarray(np.concatenate([ph, posfull[b, j * CH:(j + 1) * CH]])[None, :].astype(np.int32)),
            "consts": _consts(), "flags": flags}


def build_fused():
    B = Builder(2 * CH, CH, 2)
    B.stop_after = None
    x_ext = B.din("x_ext", [3 * CH, D])
    cvec = B.din("cvec", [D])
    pos = B.din("pos", [1, 3 * CH], I32)
    consts = B.din("consts", [128, 256])
    flags = B.din("flags", [128, 4])
    Ws = []
    for l in range(DEPTH):
        W = {n: B.din("%s_%d" % (n, l), WSHAPES[n]) for n in WNAMES}
        Ws.append(W)
    x_out = B.dout("x_out", [CH, D])
    xmid = B.scratch("xmid", [2 * CH, D], F32)
    B.setup_consts(consts, flags)
    Ws[0]['pos'] = pos
    B.layer(0, x_ext, None, xmid, B.R("xmid"), Ws[0], cvec, lambda e: e // CH, N=2 * CH, H=CH)
    Ws[1]['pos'] = pos[:, CH:3 * CH]
    B.layer(1, xmid, B.R("xmid"), x_out, B.R("x_out"), Ws[1], cvec, lambda e: 1 + e // CH, N=CH, H=CH)
    B.S.finish()
    return B


def core_inputs_fused(xfull, cfull, posfull, core):
    b, j = core // 4, core % 4
    xs, ps = [], []
    for jj in (j - 2, j - 1, j):
        if jj >= 0:
            xs.append(xfull[b, jj * CH:(jj + 1) * CH]); ps.append(posfull[b, jj * CH:(jj + 1) * CH])
        else:
            xs.append(np.zeros((CH, D), np.float32)); ps.append(np.zeros(CH, np.int32))
    flags = np.ones((128, 4), np.float32)
    flags[:, 0] = 1.0 if j >= 2 else 0.0
    flags[:, 1] = 1.0 if j >= 1 else 0.0
    return {"x_ext": np.ascontiguousarray(np.concatenate(xs, 0)), "cvec": np.ascontiguousarray(cfull[b]),
            "pos": np.ascontiguousarray(np.concatenate(ps)[None, :].astype(np.int32)), "consts": _consts(), "flags": flags}


_PROG = {}


def kernel_unfused(**inputs):
    x = np.asarray(inputs['x'], dtype=np.float32)
    c = np.asarray(inputs['c'], dtype=np.float32)
    pos = np.asarray(inputs['positions']).astype(np.int32)
    if 'single' not in _PROG:
        _PROG['single'] = build_single_layer()
    B = _PROG['single']
    cur = x
    for l in range(DEPTH):
        Wl = host_layer_weights(inputs, l)
        in_maps = []
        for core in range(NCORES):
            m = core_inputs(cur, c, pos, core)
            m.update(Wl)
            in_maps.append(m)
        res = run_bass_kernel_spmd(B.nc, in_maps, core_ids=list(range(NCORES)))
        nxt = np.empty_like(x)
        for core in range(NCORES):
            b, j = core // 4, core % 4
            nxt[b, j * CH:(j + 1) * CH] = res.results[core]["x_out"]
        cur = nxt
    return cur


def kernel(**inputs):
    x = np.asarray(inputs['x'], dtype=np.float32)
    c = np.asarray(inputs['c'], dtype=np.float32)
    pos = np.asarray(inputs['positions']).astype(np.int32)
    if 'fused' not in _PROG:
        _PROG['fused'] = build_fused()
    B = _PROG['fused']
    Wall = {}
    for l in range(DEPTH):
        for n, v in host_layer_weights(inputs, l).items():
            Wall["%s_%d" % (n, l)] = v
    in_maps = []
    for core in range(NCORES):
        m = core_inputs_fused(x, c, pos, core)
        m.update(Wall)
        in_maps.append(m)
    res = run_bass_kernel_spmd(B.nc, in_maps, core_ids=list(range(NCORES)))
    out = np.empty_like(x)
    for core in range(NCORES):
        b, j = core // 4, core % 4
        out[b, j * CH:(j + 1) * CH] = res.results[core]["x_out"]
    return out
```
